# Optimizing a Trainium2 kernel written in Bass

```python
import math
import jax
import jax.numpy as jnp
from jax import lax
import numpy as np

D_MODEL = 1024
BATCH = 4
SEQ = 8192
DEPTH = 2

GRID_W = 64
CTX_LEN = 256
HEAD_DIM = 64
NA_HEADS = 4
NA_WIN_H = 8
NA_WIN_W = 16
NA_WIDTH = NA_HEADS * HEAD_DIM
GM_GROUPS = 4
GM_CHUNK = 128
GM_WIDTH = 256
GM_GROUP_DIM = GM_WIDTH // GM_GROUPS
DA_HEADS = 4
DA_QK_DIM = 2 * HEAD_DIM
DA_V_DIM = 2 * HEAD_DIM
DA_WIDTH = DA_HEADS * DA_V_DIM
MIX_WIDTH = NA_WIDTH + GM_WIDTH + DA_WIDTH
QU_WIDTH = NA_WIDTH + DA_HEADS * DA_QK_DIM + 2 * GM_WIDTH
KV_WIDTH = 2 * NA_WIDTH + DA_HEADS * DA_QK_DIM + DA_HEADS * DA_V_DIM
IN_WIDTH = QU_WIDTH + KV_WIDTH
D_FF = -(-8 * D_MODEL // (3 * 256)) * 256
ROPE_BASE = 10000.0
EPS = 1e-6
BLOCK_Q = 128
NEG_INF = -1e30

kernel_name = 'hybrid_natten_gmlp_diffattn_dit'


def rms_norm(x, g):
    xf = x.astype(jnp.float32)
    y = xf * lax.rsqrt(jnp.mean(xf * xf, axis=-1, keepdims=True) + EPS)
    return (y * g.astype(jnp.float32)).astype(x.dtype)


def modulate(x, g, shift, scale):
    return rms_norm(x, g) * (1 + scale[:, None]) + shift[:, None]


def axial_rope(n):
    t = jnp.arange(n, dtype=jnp.int32)
    row = (t // GRID_W).astype(jnp.float32)
    col = (t % GRID_W).astype(jnp.float32)
    half = HEAD_DIM // 2
    inv = ROPE_BASE ** (-jnp.arange(0, half, 2, dtype=jnp.float32) / half)
    ar = row[:, None] * inv[None, :]
    ac = col[:, None] * inv[None, :]
    ang = jnp.concatenate([ar, ar, ac, ac], axis=-1)
    return jnp.cos(ang), jnp.sin(ang)


def apply_rope(x, cos, sin):
    xr = x.reshape(x.shape[:-1] + (2, 2, HEAD_DIM // 4))
    x1 = xr[..., 0, :]
    x2 = xr[..., 1, :]
    rot = jnp.stack([-x2, x1], axis=-2).reshape(x.shape)
    return x * cos.astype(x.dtype) + rot * sin.astype(x.dtype)


def split_qu(qu):
    lead = qu.shape[:-1]
    qa = qu[..., :NA_WIDTH].reshape(lead + (NA_HEADS, HEAD_DIM))
    o = NA_WIDTH
    qd = qu[..., o:o + DA_HEADS * DA_QK_DIM].reshape(lead + (DA_HEADS, 2, HEAD_DIM))
    o = o + DA_HEADS * DA_QK_DIM
    uv = qu[..., o:o + 2 * GM_WIDTH]
    return qa, qd, uv


def split_kv(kv):
    lead = kv.shape[:-1]
    ka = kv[..., :NA_WIDTH].reshape(lead + (NA_HEADS, HEAD_DIM))
    va = kv[..., NA_WIDTH:2 * NA_WIDTH].reshape(lead + (NA_HEADS, HEAD_DIM))
    o = 2 * NA_WIDTH
    kd = kv[..., o:o + DA_HEADS * DA_QK_DIM].reshape(lead + (DA_HEADS, 2, HEAD_DIM))
    o = o + DA_HEADS * DA_QK_DIM
    vd = kv[..., o:o + DA_HEADS * DA_V_DIM].reshape(lead + (DA_HEADS, DA_V_DIM))
    return ka, va, kd, vd


def neighborhood_attention(q, k, v, k_c, v_c, rpb):
    b, n, h, d = q.shape
    rows = n // GRID_W
    kh = min(NA_WIN_H, rows)
    kw = NA_WIN_W
    r = jnp.arange(rows)
    key_rows = jnp.clip(r - kh // 2, 0, rows - kh)[:, None] + jnp.arange(kh)[None, :]
    cq = jnp.arange(GRID_W)
    c0 = jnp.clip(cq - kw // 2, 0, GRID_W - kw)
    col_in = (cq[None, :] >= c0[:, None]) & (cq[None, :] < c0[:, None] + kw)
    qg = q.reshape(b, rows, GRID_W, h, d)
    kg = k.reshape(b, rows, GRID_W, h, d)[:, key_rows]
    vg = v.reshape(b, rows, GRID_W, h, d)[:, key_rows]
    scale = d ** -0.5
    s_win = jnp.einsum('brqhd,brjkhd->bhrqjk', qg, kg, preferred_element_type=jnp.float32) * scale
    roff = key_rows - r[:, None] + (NA_WIN_H - 1)
    coff = jnp.clip(cq[None, :] - cq[:, None], -(kw - 1), kw - 1) + (NA_WIN_W - 1)
    bias = rpb.astype(jnp.float32)[:, roff[:, None, :, None], coff[None, :, None, :]]
    s_win = jnp.where(col_in[:, None, :], s_win + bias[None], NEG_INF)
    s_ctx = jnp.einsum('brqhd,blhd->bhrql', qg, k_c, preferred_element_type=jnp.float32) * scale
    n_win = kh * GRID_W
    s = jnp.concatenate([s_win.reshape(b, h, rows, GRID_W, n_win), s_ctx], axis=-1)
    p = jax.nn.softmax(s, axis=-1).astype(v.dtype)
    p_win = p[..., :n_win].reshape(b, h, rows, GRID_W, kh, GRID_W)
    o = jnp.einsum('bhrqjk,brjkhd->brqhd', p_win, vg) + jnp.einsum('bhrql,blhd->brqhd', p[..., n_win:], v_c)
    return o.reshape(b, n, h * d)


def dense_attention(q, k, v):
    b, nq, h, d = q.shape
    s = jnp.einsum('bqhd,bkhd->bhqk', q, k, preferred_element_type=jnp.float32) * (d ** -0.5)
    p = jax.nn.softmax(s, axis=-1).astype(v.dtype)
    return jnp.einsum('bhqk,bkhd->bqhd', p, v).reshape(b, nq, h * d)


def chunk_spatial_gating(uv, v_g, ws, bs):
    b, n, _ = uv.shape
    z = jax.nn.gelu(uv)
    u = z[..., :GM_WIDTH]
    v = rms_norm(z[..., GM_WIDTH:], v_g)
    vc = v.reshape(b, n // GM_CHUNK, GM_CHUNK, GM_GROUPS, GM_GROUP_DIM)
    s = jnp.einsum('gts,bcsgd->bctgd', ws, vc) + bs.T[None, None, :, :, None]
    return u * s.reshape(b, n, GM_WIDTH)


def diff_attend(q, k, v, lam):
    s = jnp.einsum('bqhmd,bkhmd->bhmqk', q, k, preferred_element_type=jnp.float32) * (HEAD_DIM ** -0.5)
    p = jax.nn.softmax(s, axis=-1)
    a = p[:, :, 0] - lam * p[:, :, 1]
    return jnp.einsum('bhqk,bkhe->bqhe', a.astype(v.dtype), v)


def diff_attention_blocks(q, k, v, k_c, v_c, lam):
    b, n, h, _, d = q.shape
    k_all = jnp.concatenate([k, k_c], axis=1)
    v_all = jnp.concatenate([v, v_c], axis=1)
    nb = n // BLOCK_Q
    qb = jnp.moveaxis(q.reshape(b, nb, BLOCK_Q, h, 2, d), 1, 0)
    o = lax.map(lambda qblk: diff_attend(qblk, k_all, v_all, lam), qb)
    return jnp.moveaxis(o, 0, 1).reshape(b, n, h, DA_V_DIM)


def diff_head_out(o, sub_g, lam_init):
    b, n = o.shape[0], o.shape[1]
    return (rms_norm(o, sub_g) * (1 - lam_init)).reshape(b, n, DA_WIDTH)


def swiglu(h, w1, w3, w2):
    return (jax.nn.silu(h @ w1) * (h @ w3)) @ w2


def hybrid_layer(x, xc, c, c_ctx, lam_init, update_ctx,
                 w_mod, b_mod, norm1_g, w_in, na_q_g, na_k_g, na_rpb,
                 gm_v_g, gm_ws, gm_bs, da_q_g, da_k_g, da_lq1, da_lk1, da_lq2, da_lk2,
                 da_sub_g, w_out, norm2_g, ffn_w1, ffn_w3, ffn_w2):
    b, n, _ = x.shape
    mod = (jax.nn.silu(c) @ w_mod + b_mod).reshape(b, 6, D_MODEL)
    mod_c = (jax.nn.silu(c_ctx) @ w_mod + b_mod).reshape(1, 6, D_MODEL)
    h = modulate(x, norm1_g, mod[:, 0], mod[:, 1])
    hc = modulate(xc, norm1_g, mod_c[:, 0], mod_c[:, 1])
    proj = h @ w_in
    qa, qd, uv = split_qu(proj[..., :QU_WIDTH])
    ka, va, kd, vd = split_kv(proj[..., QU_WIDTH:])
    if update_ctx:
        proj_c = hc @ w_in
        kv_c = proj_c[..., QU_WIDTH:]
    else:
        kv_c = hc @ w_in[:, QU_WIDTH:]
    ka_c, va_c, kd_c, vd_c = split_kv(kv_c)
    qa = rms_norm(qa, na_q_g)
    ka = rms_norm(ka, na_k_g)
    ka_c = rms_norm(ka_c, na_k_g)
    cos, sin = axial_rope(n)
    cos = cos[:, None, None, :]
    sin = sin[:, None, None, :]
    qd = apply_rope(rms_norm(qd, da_q_g), cos, sin)
    kd = apply_rope(rms_norm(kd, da_k_g), cos, sin)
    kd_c = rms_norm(kd_c, da_k_g)
    f32 = jnp.float32
    lam = (jnp.exp(jnp.sum(da_lq1.astype(f32) * da_lk1.astype(f32)))
           - jnp.exp(jnp.sum(da_lq2.astype(f32) * da_lk2.astype(f32))) + lam_init)
    o_a = neighborhood_attention(qa, ka, va, ka_c, va_c, na_rpb)
    o_b = chunk_spatial_gating(uv, gm_v_g, gm_ws, gm_bs)
    o_c = diff_head_out(diff_attention_blocks(qd, kd, vd, kd_c, vd_c, lam), da_sub_g, lam_init)
    mixed = jnp.concatenate([o_a, o_b, o_c], axis=-1) @ w_out
    x = x + mod[:, 2][:, None] * mixed
    x = x + mod[:, 5][:, None] * swiglu(modulate(x, norm2_g, mod[:, 3], mod[:, 4]), ffn_w1, ffn_w3, ffn_w2)
    if update_ctx:
        qa_c, qd_c, uv_c = split_qu(proj_c[..., :QU_WIDTH])
        oa_c = dense_attention(rms_norm(qa_c, na_q_g), ka_c, va_c)
        ob_c = chunk_spatial_gating(uv_c, gm_v_g, gm_ws, gm_bs)
        oc_c = diff_head_out(diff_attend(rms_norm(qd_c, da_q_g), kd_c, vd_c, lam), da_sub_g, lam_init)
        mixed_c = jnp.concatenate([oa_c, ob_c, oc_c], axis=-1) @ w_out
        xc = xc + mod_c[:, 2][:, None] * mixed_c
        xc = xc + mod_c[:, 5][:, None] * swiglu(modulate(xc, norm2_g, mod_c[:, 3], mod_c[:, 4]), ffn_w1, ffn_w3, ffn_w2)
    return x, xc


def setup_inputs(seed: int = 0) -> dict:
    key = jax.random.key(seed)
    ks = jax.random.split(key, 32)

    def nrm(k, shape, s):
        return jax.random.normal(k, shape, jnp.float32) * s

    L = DEPTH
    return {
        'x': nrm(ks[0], (BATCH, SEQ, D_MODEL), 1.0),
        'c': nrm(ks[1], (BATCH, D_MODEL), 1.0),
        'ctx': nrm(ks[2], (BATCH, CTX_LEN, D_MODEL), 1.0),
        'c_ctx': nrm(ks[3], (D_MODEL,), 1.0),
        'w_mod': nrm(ks[4], (L, D_MODEL, 6 * D_MODEL), 0.5 * D_MODEL ** -0.5),
        'b_mod': nrm(ks[5], (L, 6 * D_MODEL), 0.02),
        'norm1_g': 1.0 + nrm(ks[6], (L, D_MODEL), 0.05),
        'w_in': nrm(ks[7], (L, D_MODEL, IN_WIDTH), D_MODEL ** -0.5),
        'na_q_g': 1.0 + nrm(ks[8], (L, HEAD_DIM), 0.05),
        'na_k_g': 1.0 + nrm(ks[9], (L, HEAD_DIM), 0.05),
        'na_rpb': nrm(ks[10], (L, NA_HEADS, 2 * NA_WIN_H - 1, 2 * NA_WIN_W - 1), 0.1),
        'gm_v_g': 1.0 + nrm(ks[11], (L, GM_WIDTH), 0.05),
        'gm_ws': nrm(ks[12], (L, GM_GROUPS, GM_CHUNK, GM_CHUNK), GM_CHUNK ** -0.5),
        'gm_bs': 1.0 + nrm(ks[13], (L, GM_GROUPS, GM_CHUNK), 0.02),
        'da_q_g': 1.0 + nrm(ks[14], (L, HEAD_DIM), 0.05),
        'da_k_g': 1.0 + nrm(ks[15], (L, HEAD_DIM), 0.05),
        'da_lq1': nrm(ks[16], (L, HEAD_DIM), 0.1),
        'da_lk1': nrm(ks[17], (L, HEAD_DIM), 0.1),
        'da_lq2': nrm(ks[18], (L, HEAD_DIM), 0.1),
        'da_lk2': nrm(ks[19], (L, HEAD_DIM), 0.1),
        'da_sub_g': 1.0 + nrm(ks[20], (L, DA_V_DIM), 0.05),
        'w_out': nrm(ks[21], (L, MIX_WIDTH, D_MODEL), MIX_WIDTH ** -0.5),
        'norm2_g': 1.0 + nrm(ks[22], (L, D_MODEL), 0.05),
        'ffn_w1': nrm(ks[23], (L, D_MODEL, D_FF), D_MODEL ** -0.5),
        'ffn_w3': nrm(ks[24], (L, D_MODEL, D_FF), D_MODEL ** -0.5),
        'ffn_w2': nrm(ks[25], (L, D_FF, D_MODEL), D_FF ** -0.5),
    }


def reference(x, c, ctx, c_ctx, w_mod, b_mod, norm1_g, w_in, na_q_g, na_k_g, na_rpb,
              gm_v_g, gm_ws, gm_bs, da_q_g, da_k_g, da_lq1, da_lk1, da_lq2, da_lk2,
              da_sub_g, w_out, norm2_g, ffn_w1, ffn_w3, ffn_w2):
    xc = ctx
    for i in range(DEPTH):
        lam_init = 0.8 - 0.6 * math.exp(-0.3 * i)
        x, xc = hybrid_layer(
            x, xc, c, c_ctx, lam_init, i < DEPTH - 1,
            w_mod[i], b_mod[i], norm1_g[i], w_in[i], na_q_g[i], na_k_g[i], na_rpb[i],
            gm_v_g[i], gm_ws[i], gm_bs[i], da_q_g[i], da_k_g[i], da_lq1[i], da_lk1[i],
            da_lq2[i], da_lk2[i], da_sub_g[i], w_out[i], norm2_g[i], ffn_w1[i], ffn_w3[i], ffn_w2[i])
    return x
```

```python
import contextlib
import math
import numpy as np
import concourse.bass as bass
import concourse.mybir as mybir
from concourse.bass_utils import run_bass_kernel_spmd

F32 = mybir.dt.float32
BF16 = mybir.dt.bfloat16
AF = mybir.ActivationFunctionType
ALU = mybir.AluOpType

ENGS = ("pe", "act", "dve", "pool", "sp")
D = 1024
CTX = 256
INW = 2816
DFF = 2816
EPS = 1e-6
PAIRS = [[0, 1], [2, 3], [4, 5], [6, 7]]


class Op:
    __slots__ = ("eng", "fn", "waits", "dma", "dsem", "signal", "count", "dinc")

    def __init__(self, eng, fn, dma=False):
        self.eng = eng
        self.fn = fn
        self.waits = []
        self.dma = dma
        self.dsem = None
        self.signal = False
        self.count = 0
        self.dinc = 16


class Prog:
    def __init__(self, nc):
        self.nc = nc
        self.outer = contextlib.ExitStack()
        self.sems = {}
        self.dcum = {}
        self.cnt = {e: 0 for e in ENGS}
        self.waited = {e: {} for e in ENGS}
        self.stack = None
        self.nalloc = 0
        self.hard = False
        self.begin()

    def begin(self):
        self.ops = {e: [] for e in ENGS}
        self.track = {}
        self.stack = contextlib.ExitStack()

    def sbuf(self, name, shape, dtype, outer=False):
        st = self.outer if outer else self.stack
        self.nalloc += 1
        return st.enter_context(self.nc.sbuf_tensor(f"{name}_s{self.nalloc}", list(shape), dtype))

    def psum(self, name, shape, dtype=F32):
        self.nalloc += 1
        return self.stack.enter_context(self.nc.psum_tensor(f"{name}_p{self.nalloc}", list(shape), dtype))

    def sem(self, name):
        if name not in self.sems:
            self.sems[name] = self.outer.enter_context(self.nc.semaphore(name))
        return self.sems[name]

    def _dep(self, op, prod):
        if prod is None or prod is op:
            return
        if prod.dma:
            op.waits.append((prod.dsem, self.dcum[prod.dsem]))
        else:
            if prod.eng == op.eng and (not self.hard or op.eng == "pe"):
                return
            prod.signal = True
            op.waits.append(prod)

    def add(self, eng, fn, reads=(), writes=(), dsem=None, dinc=16):
        dma = dsem is not None
        op = Op(eng, fn, dma)
        for k in reads:
            t = self.track.get(k)
            if t is not None:
                self._dep(op, t[0])
        for k in writes:
            t = self.track.get(k)
            if t is not None:
                self._dep(op, t[0])
                for r in t[1].values():
                    self._dep(op, r)
        if dma:
            op.dsem = dsem
            self.dcum[dsem] = self.dcum.get(dsem, 0) + dinc
            op.dinc = dinc
        for k in reads:
            t = self.track.setdefault(k, [None, {}])
            t[1][eng if not dma else ("dma", dsem)] = op
        for k in writes:
            self.track[k] = [op, {}]
        self.ops[eng].append(op)
        return op

    def end(self, barrier=True):
        nc = self.nc
        for e in ENGS:
            for op in self.ops[e]:
                if op.signal and not op.dma:
                    self.cnt[e] += 1
                    op.count = self.cnt[e]
        esem = {e: self.sem("s_" + e) for e in ENGS}
        for d in self.dcum:
            self.sem(d)
        sems = self.sems

        def run(en):
            def body(eng):
                waited = self.waited[en]
                for op in self.ops[en]:
                    for w in op.waits:
                        if isinstance(w, tuple):
                            sname, val = w
                        else:
                            sname, val = "s_" + w.eng, w.count
                        if waited.get(sname, 0) >= val:
                            continue
                        waited[sname] = val
                        eng.wait_ge(sems[sname], val)
                    ins = op.fn(eng)
                    if op.dma:
                        ins.then_inc(sems[op.dsem], op.dinc)
                    elif op.signal:
                        ins.then_inc(esem[en], 1)
                if en == "sp":
                    for d, c in self.dcum.items():
                        if waited.get(d, 0) < c:
                            waited[d] = c
                            eng.wait_ge(sems[d], c)
            return body

        with nc.Block() as block:
            block.tensor(run("pe"))
            block.scalar(run("act"))
            block.vector(run("dve"))
            block.gpsimd(run("pool"))
            block.sync(run("sp"))
        if barrier:
            nc.all_engine_barrier()
        self.stack.close()
        self.begin()

    def close(self):
        self.stack.close()
        self.outer.close()


def build(S, nlayers=2, debug=False, stop_after=None, gpad=None):
    NT = S // 2
    NG = NT // 512
    NTX = NT + CTX
    KT = S // 128
    NTT = NT // 128
    NB = NTT + 4
    NSET = 1 if NG == 1 else 3
    nc = bass.Bass("TRN2", target_bir_lowering=False)

    def din(name, shape, dt=F32):
        return nc.dram_tensor(name, list(shape), dt, kind="ExternalInput").ap()

    x_in = din("x", [NT, D])
    ctx_in = din("ctx", [CTX, D])
    cvec = din("cvec", [16, 128])
    vecs = din("vecs", [nlayers, 128, 128])
    lamv = din("lamv", [nlayers, 1, 256])
    vg_in = din("vg", [nlayers, 1, 256])
    w_mod = din("w_mod", [nlayers, D, 6 * D])
    w_in = din("w_in", [nlayers, D, INW])
    w_out = din("w_out", [nlayers, D, D])
    w1 = din("w1", [nlayers, D, DFF])
    w3 = din("w3", [nlayers, D, DFF])
    w2 = din("w2", [nlayers, DFF, D])
    wsT_in = din("wsT", [nlayers, 4, 128, 128])
    cos_in = din("cos", [128, NT])
    sin_in = din("sin", [128, NT])
    nab = din("nab", [nlayers, 4, NSET * 8, 128, 512])
    ident_in = din("ident", [128, 128])
    rperm_in = din("rperm", [128, 128])
    blk_in = din("blk", [128, 128])
    out = nc.dram_tensor("out", [NT, D], F32, kind="ExternalOutput").ap()

    def scratch(name, shape, dt):
        if debug:
            return nc.dram_tensor(name, list(shape), dt, kind="ExternalOutput")
        return nc.dram_tensor(name, list(shape), dt)

    xT = scratch("xT", [D, NTX], F32).ap()
    qT = scratch("qT", [768, NTX], BF16).ap()
    kT_in_t = [nc.dram_tensor(f"kT_in{g}", [768, 512], BF16) for g in range(NG)]
    kT_all_t = [nc.dram_tensor(f"kT_all{g}", [2 * 768, 512], BF16) for g in range(NG)]
    V_in_t = [nc.dram_tensor(f"V_in{g}", [512, 768], BF16) for g in range(NG)]
    V_all_t = [nc.dram_tensor(f"V_all{g}", [2 * 512, 768], BF16) for g in range(NG)]
    kT_in = [t.ap() for t in kT_in_t]
    kT_all = [t.ap() for t in kT_all_t]
    V_in = [t.ap() for t in V_in_t]
    V_all = [t.ap() for t in V_all_t]
    kTc = scratch("kTc", [768, CTX], BF16).ap()
    Vc = scratch("Vc", [CTX, 768], BF16).ap()
    mixT = scratch("mixT", [D, NTX], BF16).ap()
    h2T = scratch("h2T", [D, NTX], BF16).ap()
    gT = scratch("gT", [DFF, NTX], BF16).ap()
    if debug:
        dbg_k = nc.dram_tensor("dbg_k", [2 * 768, 512], BF16, kind="ExternalOutput").ap()
        dbg_v = nc.dram_tensor("dbg_v", [2 * 512, 768], BF16, kind="ExternalOutput").ap()

    P = Prog(nc)
    A = P.add

    identf = P.sbuf("identf", [128, 128], F32, outer=True)
    identb = P.sbuf("identb", [128, 128], BF16, outer=True)
    rperm = P.sbuf("rperm", [128, 128], BF16, outer=True)
    blk = P.sbuf("blk", [128, 128], BF16, outer=True)
    onesb = P.sbuf("onesb", [128, 128], BF16, outer=True)
    onesf = P.sbuf("onesf", [128, 128], F32, outer=True)
    vT = P.sbuf("vT", [128, 128], F32, outer=True)
    modT = P.sbuf("modT", [128, 48, 2], F32, outer=True)
    A1 = P.sbuf("A1", [128, 8, 2], F32, outer=True)
    A2 = P.sbuf("A2", [128, 8, 2], F32, outer=True)
    sgl = P.sbuf("sgl", [128, 1], F32, outer=True)
    nlam = P.sbuf("nlam", [128, 1], F32, outer=True)
    vgB = P.sbuf("vgB", [128, 256], F32, outer=True)
    scT = P.sbuf("scT", [128, 8, 2], F32, outer=True)

    groups = [(g, 512, g * 512, False) for g in range(NG)] + [(NG, CTX, NT, True)]

    def mm(o, l, r, st, sp, rd, wr):
        A("pe", lambda e: e.matmul(o, lhsT=l, rhs=r, start=st, stop=sp), rd, wr)

    def tr(o, i, idn, rd, wr):
        A("pe", lambda e: e.transpose(out=o, in_=i, identity=idn), rd, wr)

    def act(o, i, f, rd, wr, scale=1.0, bias=0.0, eng="act"):
        A(eng, lambda e: e.activation(out=o, in_=i, func=f, scale=scale, bias=bias), rd, wr)

    def cp(eng, o, i, rd, wr):
        if eng == "act":
            A("act", lambda e: e.copy(out=o, in_=i), rd, wr)
        else:
            A(eng, lambda e: e.tensor_copy(out=o, in_=i), rd, wr)

    def tt(eng, o, a, b, op, rd, wr):
        A(eng, lambda e: e.tensor_tensor(out=o, in0=a, in1=b, op=op), rd, wr)

    def ts(eng, o, a, s1, s2, op0, op1, rd, wr):
        if s2 is None:
            A(eng, lambda e: e.tensor_scalar(out=o, in0=a, scalar1=s1, scalar2=None, op0=op0), rd, wr)
        else:
            A(eng, lambda e: e.tensor_scalar(out=o, in0=a, scalar1=s1, scalar2=s2, op0=op0, op1=op1), rd, wr)

    def stt(o, a, sc, b, op0, op1, rd, wr):
        A("dve", lambda e: e.scalar_tensor_tensor(out=o, in0=a, scalar=sc, in1=b, op0=op0, op1=op1), rd, wr)

    def dma(o, i, rd, wr, sem, eng="sp"):
        A(eng, lambda e: e.dma_start(out=o, in_=i), rd, wr, dsem=sem)

    def rstd_from_sum(o, ps, n, rd, wr, tmp):
        act(tmp, ps, AF.Ln, rd, [wr + "_t"], scale=1.0 / n, bias=EPS)
        act(o, tmp, AF.Exp, [wr + "_t"], [wr], scale=-0.5)

    P.hard = True
    st32 = P.sbuf("c_st", [128, 3, 128], F32)
    dma(identf[:], ident_in, [], ["identf"], "d_c0")
    dma(st32[:, 0, :], ident_in, [], ["st0"], "d_c1")
    dma(st32[:, 1, :], rperm_in, [], ["st1"], "d_c1")
    dma(st32[:, 2, :], blk_in, [], ["st2"], "d_c1")
    cp("dve", identb[:], st32[:, 0, :], ["st0"], ["identb"])
    cp("dve", rperm[:], st32[:, 1, :], ["st1"], ["rperm"])
    cp("dve", blk[:], st32[:, 2, :], ["st2"], ["blk"])
    A("dve", lambda e: e.memset(onesb[:], 1.0), [], ["onesb"])
    A("dve", lambda e: e.memset(onesf[:], 1.0), [], ["onesf"])
    cv = P.sbuf("cv", [16, 128], F32)
    ps_c = P.psum("ps_c", [128, 16])
    dma(cv[:], cvec, [], ["cv"], "d_c2")
    tr(ps_c[:, :], cv[:], identf[0:16, 0:16], ["cv", "identf"], ["ps_c"])
    act(scT[:].rearrange("p k s -> p s k"), ps_c[:].rearrange("p (s k) -> p s k", s=2), AF.Silu, ["ps_c"], ["scT"])
    P.end()
    P.hard = False

    xin = [P.sbuf(f"xin{i}", [128, D], F32) for i in range(2)]
    xg_t = [P.sbuf(f"xgT{i}", [128, 8, 512], F32) for i in range(2)]
    ps_t = [P.psum(f"ps_t{i}", [128, 512]) for i in range(4)]
    n = 0
    for (g, W, off, isctx) in groups:
        xb = xg_t[g % 2]
        for t4 in range(W // 128):
            src = ctx_in[t4 * 128:(t4 + 1) * 128, :] if isctx else x_in[off + t4 * 128: off + (t4 + 1) * 128, :]
            xi = xin[n % 2]
            dma(xi[:], src, [], [f"xin{n % 2}"], f"d_xin{n % 2}")
            for b in range(2):
                pb = ps_t[(n % 2) * 2 + b]
                for c4 in range(4):
                    c = b * 4 + c4
                    tr(pb[:, c4 * 128:(c4 + 1) * 128], xi[:, c * 128:(c + 1) * 128], identf[:], [f"xin{n % 2}", "identf"], [f"ps_t{(n % 2) * 2 + b}"])
                cp("act" if b == 0 else "dve", xb[:, b * 4:(b + 1) * 4, t4 * 128:(t4 + 1) * 128], pb[:].rearrange("p (c n) -> p c n", c=4),
                   [f"ps_t{(n % 2) * 2 + b}"], [f"xgT{g % 2}"])
            n += 1
        dma(xT[:, off:off + W].rearrange("(c p) n -> p c n", p=128), xb[:, :, 0:W], [f"xgT{g % 2}"], [("xT", g)], f"d_xgT{g % 2}")
    P.end()
    if stop_after == "T":
        P.close()
        return nc

    for l in range(nlayers):
        last = (l == nlayers - 1)
        lam_init = 0.8 - 0.6 * math.exp(-0.3 * l)
        act_groups = groups[:NG] if last else groups

        P.hard = True
        vraw = P.sbuf("vraw", [128, 128], F32)
        ps_v = P.psum("ps_v", [128, 128])
        dma(vraw[:], vecs[l], [], ["vraw"], "d_m0")
        tr(ps_v[:, :], vraw[:], identf[:], ["vraw"], ["ps_v"])
        cp("act", vT[:], ps_v[:], ["ps_v"], ["vT"])
        ts("dve", sgl[:], vT[:, 68:69], 1.0 - lam_init, None, ALU.mult, None, ["vT"], ["sgl"])
        lv = P.sbuf("lv", [1, 256], F32)
        lacc = P.sbuf("lacc", [1, 32], F32)
        lacc0 = P.sbuf("lacc0", [1, 4], F32)
        A("dve", lambda e: e.memset(lacc[:], 0.0), [], ["lacc"])
        ljunk = P.sbuf("ljunk", [1, 64], F32)
        ps_l = P.psum("ps_l", [128, 2])
        dma(lv[:], lamv[l], [], ["lv"], "d_m1")
        for i in range(2):
            tt("dve", ljunk[:], lv[:, (2 * i) * 64:(2 * i + 1) * 64], lv[:, (2 * i + 1) * 64:(2 * i + 2) * 64], ALU.mult, ["lv"], ["ljunk"])
            A("dve", lambda e, i=i: e.tensor_reduce(out=lacc0[:, i:i + 1], in_=ljunk[:], axis=mybir.AxisListType.X, op=ALU.add), ["ljunk"], ["lacc0"])
        cp("dve", lacc[:, 0:2], lacc0[:, 0:2], ["lacc0"], ["lacc"])
        act(lacc[:, 0:32], lacc[:, 0:32], AF.Exp, ["lacc"], ["lacc"])
        tt("dve", lacc[:, 2:3], lacc[:, 1:2], lacc[:, 0:1], ALU.subtract, ["lacc"], ["lacc"])
        ts("dve", lacc[:, 2:4], lacc[:, 2:4], 1.0, -lam_init, ALU.mult, ALU.add, ["lacc"], ["lacc"])
        mm(ps_l[:, 0:2], onesf[0:1, :], lacc[:, 2:4], True, True, ["lacc", "onesf"], ["ps_l"])
        cp("act", nlam[:], ps_l[:, 0:1], ["ps_l"], ["nlam"])
        vgr = P.sbuf("vgr", [1, 256], F32)
        ps_g = P.psum("ps_g", [128, 256])
        dma(vgr[:], vg_in[l], [], ["vgr"], "d_m2")
        mm(ps_g[:, :], onesf[0:1, :], vgr[:], True, True, ["vgr", "onesf"], ["ps_g"])
        cp("act", vgB[:], ps_g[:], ["ps_g"], ["vgB"])
        wm = [P.sbuf(f"wm{i}", [128, 8, 1024], F32) for i in range(2)]
        ps_m = P.psum("ps_m", [128, 96])
        for j in range(6):
            wb_ = wm[j % 2]
            for hh in range(2):
                dma(wb_[:, hh * 4:(hh + 1) * 4, :], w_mod[l][hh * 512:(hh + 1) * 512, j * 1024:(j + 1) * 1024].rearrange("(c p) n -> p c n", p=128),
                    [], [f"wm{j % 2}"], f"d_wm{j % 2}")
            for c in range(8):
                ct = j * 8 + c
                for kc in range(8):
                    mm(ps_m[:, ct * 2:ct * 2 + 2], wb_[:, kc, c * 128:(c + 1) * 128], scT[:, kc, :], kc == 0, kc == 7, [f"wm{j % 2}", "scT"], ["ps_m"])
        for s in range(2):
            tt("dve", modT[:, :, s], ps_m[:].rearrange("p (c s) -> p c s", s=2)[:, :, s], vT[:, 0:48], ALU.add, ["ps_m", "vT"], ["modT"])
        for s in range(2):
            stt(A1[:, :, s], modT[:, 8:16, s], 1.0, vT[:, 48:56], ALU.add, ALU.mult, ["modT", "vT"], ["A1"])
            stt(A2[:, :, s], modT[:, 32:40, s], 1.0, vT[:, 56:64], ALU.add, ALU.mult, ["modT", "vT"], ["A2"])
        P.end()
        P.hard = False

        def Bv(j, c, s):
            return modT[:, j * 8 + c, s:s + 1]

        def norm_mod(xg, W, Aq, jshift, s, hT, sq, rs, tmpb, ps_st, key, pskey, rst):
            for c in range(8):
                tt("pool", sq[:, c, 0:W], xg[:, c, 0:W], xg[:, c, 0:W], ALU.mult, [key], ["sq"])
            for c in range(8):
                mm(ps_st[:, 0:W], onesb[:], sq[:, c, 0:W], c == 0, c == 7, ["sq", "onesb"], [pskey])
            rstd_from_sum(rs[:, 0:W], ps_st[:, 0:W], float(D), [pskey], "rs", rst[:, 0:W])
            for c in range(8):
                tb = tmpb[c % 2]
                tt("dve", tb[:, 0:W], xg[:, c, 0:W], rs[:, 0:W], ALU.mult, [key, "rs"], [f"tmpb{c % 2}"])
                act(hT[:, c, 0:W], tb[:, 0:W], AF.Identity, [f"tmpb{c % 2}"], ["hT"], scale=Aq[:, c, s:s + 1], bias=Bv(jshift, c, s))

        wi = P.sbuf("wi", [128, 8, INW], BF16)
        for c in range(8):
            for hh in range(2):
                dma(wi[:, c, hh * 1408:(hh + 1) * 1408], w_in[l][c * 128:(c + 1) * 128, hh * 1408:(hh + 1) * 1408], [], ["wi"], "d_wi", eng="pool")
        wsr = P.sbuf("wsr", [128, 4, 128], BF16)
        dma(wsr[:], wsT_in[l].rearrange("g s t -> s g t"), [], ["wsr"], "d_ws", eng="pool")
        xg_p = [P.sbuf(f"xg{i}", [128, 8, 512], F32) for i in range(2)]
        cs_p = [P.sbuf(f"cs{i}", [128, 2, 512], F32) for i in range(2)]
        sq = P.sbuf("sq", [128, 8, 512], BF16)
        rs = P.sbuf("rs", [128, 512], F32)
        tmpb = [P.sbuf(f"tmpb{i}", [128, 512], F32) for i in range(2)]
        hT = P.sbuf("hT", [128, 8, 512], BF16)
        qf = [P.sbuf(f"qf{i}", [128, 512], F32) for i in range(2)]
        sqq_ = [P.sbuf(f"sqq{i}", [128, 512], BF16) for i in range(2)]
        lnq_ = [P.sbuf(f"lnq{i}", [128, 512], F32) for i in range(2)]
        rr_ = [P.sbuf(f"rr{i}", [128, 512], F32) for i in range(2)]
        qn_ = [P.sbuf(f"qn{i}", [128, 512], F32) for i in range(2)]
        qnb_ = [P.sbuf(f"qnb{i}", [128, 512], BF16) for i in range(2)]
        t1_ = [P.sbuf(f"t1{i}", [128, 512], F32) for i in range(2)]
        t2_ = [P.sbuf(f"t2{i}", [128, 512], F32) for i in range(2)]
        qo = [P.sbuf(f"qo{i}", [128, 512], BF16) for i in range(4)]
        z_ = [P.sbuf(f"z{i}", [128, 512], F32) for i in range(2)]
        zjunk_ = [P.sbuf(f"zjunk{i}", [128, 256], F32) for i in range(2)]
        ssv_ = [P.sbuf(f"ssv{i}", [128, 32], F32) for i in range(2)]
        ssw_ = [P.sbuf(f"ssw{i}", [128, 32], F32) for i in range(2)]
        ssv0_ = [P.sbuf(f"ssv0{i}", [128, 2], F32) for i in range(2)]
        for i in range(2):
            A("dve", lambda e, i=i: e.memset(ssv_[i][:], 1.0), [], [f"ssv{i}"])
        vnb_ = [P.sbuf(f"vnb{i}", [128, 256], BF16) for i in range(2)]
        ob_ = [P.sbuf(f"ob{i}", [128, 256], BF16) for i in range(2)]
        obT = P.sbuf("obT", [128, 2, 512], BF16)
        vt = [P.sbuf(f"vt{i}", [128, 768], BF16) for i in range(2)]
        rst = P.sbuf("rst", [128, 512], F32)
        ps_q = [P.psum(f"ps_q{i}", [128, 512]) for i in range(2)]
        ps_n = P.psum("ps_n", [128, 512])
        ps_r = P.psum("ps_r", [128, 512])
        ps_uv = P.psum("ps_uv", [128, 512])
        ps_va = P.psum("ps_va", [128, 512])
        ps_vc = P.psum("ps_vc", [128, 512])
        ps_tb = P.psum("ps_tb", [128, 256], BF16)

        ftiles = [(0, "a", 64, "q", 0), (128, "a", 64, "q", 1)] + [(256 + 128 * h, "c", 66, "q", 2 + h) for h in range(4)] + \
                 [(1280, "a", 65, "k", 0), (1408, "a", 65, "k", 1)] + [(1792 + 128 * h, "c", 67, "k", 2 + h) for h in range(4)]

        def load_group(gi):
            g, W, off, isctx = groups[gi]
            b = gi % 2
            dma(xg_p[b][:, :, 0:W], xT[:, off:off + W].rearrange("(c p) n -> p c n", p=128), [("xT", g)], [f"xg{b}"], f"d_xg{b}")
            if not isctx:
                dma(cs_p[b][:, 0, :], cos_in[:, off:off + W], [], [f"cs{b}"], f"d_cs{b}")
                dma(cs_p[b][:, 1, :], sin_in[:, off:off + W], [], [f"cs{b}"], f"d_cs{b}")

        def gather(g):
            A("pool", lambda e: e.collective_compute("AllGather", ALU.bypass, replica_groups=PAIRS, ins=[kT_in_t[g].ap().opt()], outs=[kT_all_t[g].ap().opt()]),
              [("kT_in", g)], [("kT_all", g)], dsem="d_cc", dinc=1)
            A("pool", lambda e: e.collective_compute("AllGather", ALU.bypass, replica_groups=PAIRS, ins=[V_in_t[g].ap().opt()], outs=[V_all_t[g].ap().opt()]),
              [("V_in", g)], [("V_all", g)], dsem="d_cc", dinc=1)

        load_group(0)
        nq = 0
        nv = 0
        for gi, (g, W, off, isctx) in enumerate(groups):
            if gi + 1 < len(groups):
                load_group(gi + 1)
            b = gi % 2
            xg = xg_p[b]
            s = 1 if isctx else 0
            norm_mod(xg, W, A1, 0, s, hT, sq, rs, tmpb, ps_r, f"xg{b}", "ps_r", rst)
            atiles = [ft for ft in ftiles if not (isctx and last and ft[3] == "q")]

            def proj(i):
                co = atiles[i][0]
                k = (nq + i) % 2
                for c in range(8):
                    mm(ps_q[k][:, 0:W], wi[:, c, co:co + 128], hT[:, c, 0:W], c == 0, c == 7, ["wi", "hT"], [f"ps_q{k}"])

            proj(0)
            nq0 = nq
            for ti, (co, kind, gcol, dst, dti) in enumerate(atiles):
                nqi = nq0 + ti
                k2 = nqi % 2
                pq = ps_q[k2]
                pqk = f"ps_q{k2}"
                qfb = qf[k2]
                qfk = f"qf{k2}"
                cp("act", qfb[:, 0:W], pq[:, 0:W], [pqk], [qfk])
                if ti + 1 < len(atiles):
                    nq = nq0
                    proj(ti + 1)
                tt("pool", sqq_[k2][:, 0:W], qfb[:, 0:W], qfb[:, 0:W], ALU.mult, [qfk], [f"sqq{k2}"])
                mm(ps_n[:, 0:W], blk[:], sqq_[k2][:, 0:W], True, True, [f"sqq{k2}", "blk"], ["ps_n"])
                act(lnq_[k2][:, 0:W], ps_n[:, 0:W], AF.Ln, ["ps_n"], [f"lnq{k2}"], scale=1.0 / 64, bias=EPS)
                act(rr_[k2][:, 0:W], lnq_[k2][:, 0:W], AF.Exp, [f"lnq{k2}"], [f"rr{k2}"], scale=-0.5)
                qob = qo[nqi % 4]
                qok = f"qo{nqi % 4}"
                rope = (kind == "c") and not isctx
                if not rope:
                    stt(qob[:, 0:W], qfb[:, 0:W], vT[:, gcol:gcol + 1], rr_[k2][:, 0:W], ALU.mult, ALU.mult, [qfk, f"rr{k2}", "vT"], [qok])
                else:
                    stt(qn_[k2][:, 0:W], qfb[:, 0:W], vT[:, gcol:gcol + 1], rr_[k2][:, 0:W], ALU.mult, ALU.mult, [qfk, f"rr{k2}", "vT"], [f"qn{k2}"])
                    cp("pool", qnb_[k2][:, 0:W], qn_[k2][:, 0:W], [f"qn{k2}"], [f"qnb{k2}"])
                    mm(ps_r[:, 0:W], rperm[:], qnb_[k2][:, 0:W], True, True, [f"qnb{k2}", "rperm"], ["ps_r"])
                    tt("pool", t1_[k2][:, 0:W], qn_[k2][:, 0:W], cs_p[b][:, 0, 0:W], ALU.mult, [f"qn{k2}", f"cs{b}"], [f"t1{k2}"])
                    tt("dve", t2_[k2][:, 0:W], ps_r[:, 0:W], cs_p[b][:, 1, 0:W], ALU.mult, ["ps_r", f"cs{b}"], [f"t2{k2}"])
                    tt("dve", qob[:, 0:W], t1_[k2][:, 0:W], t2_[k2][:, 0:W], ALU.add, [f"t1{k2}", f"t2{k2}"], [qok])
                if dst == "q":
                    dma(qT[dti * 128:(dti + 1) * 128, off:off + W], qob[:, 0:W], [qok], [("qT", g)], f"d_qo{nqi % 4}")
                elif isctx:
                    dma(kTc[dti * 128:(dti + 1) * 128, :], qob[:, 0:W], [qok], ["kTc"], f"d_qo{nqi % 4}")
                else:
                    dma(kT_in[g][dti * 128:(dti + 1) * 128, :], qob[:, 0:W], [qok], [("kT_in", g)], f"d_qo{nqi % 4}")
            nq = nq0 + len(atiles)
            gate_on = not (isctx and last)
            nv0 = nv

            def stage_a(t4):
                tsl = slice(t4 * 128, (t4 + 1) * 128)
                kv = (nv0 + t4) % 2
                vtb, vtk = vt[kv], f"vt{kv}"
                for c in range(8):
                    mm(ps_va[:, 0:256], hT[:, c, tsl], wi[:, c, 1536:1792], c == 0, c == 7, ["wi", "hT"], ["ps_va"])
                for c in range(8):
                    mm(ps_vc[:, :], hT[:, c, tsl], wi[:, c, 2304:2816], c == 0, c == 7, ["wi", "hT"], ["ps_vc"])
                cp("act", vtb[:, 0:256], ps_va[:, 0:256], ["ps_va"], [vtk])
                cp("act", vtb[:, 256:768], ps_vc[:, :], ["ps_vc"], [vtk])
                if isctx:
                    dma(Vc[tsl, :], vtb[:], [vtk], ["Vc"], f"d_vt{kv}")
                else:
                    dma(V_in[g][t4 * 128:(t4 + 1) * 128, :], vtb[:], [vtk], [("V_in", g)], f"d_vt{kv}")
                if not gate_on:
                    return
                for c in range(8):
                    mm(ps_uv[:, :], hT[:, c, tsl], wi[:, c, 768:1280], c == 0, c == 7, ["wi", "hT"], ["ps_uv"])
                act(z_[kv][:], ps_uv[:], AF.Gelu_apprx_tanh, ["ps_uv"], [f"z{kv}"])
                tt("pool", zjunk_[kv][:], z_[kv][:, 256:512], z_[kv][:, 256:512], ALU.mult, [f"z{kv}"], [f"zjunk{kv}"])
                A("dve", lambda e, k=kv: e.tensor_reduce(out=ssv0_[k][:, 0:1], in_=zjunk_[k][:], axis=mybir.AxisListType.X, op=ALU.add), [f"zjunk{kv}"], [f"ssv0{kv}"])
                P.hard = True
                cp("dve", ssv_[kv][:, 0:1], ssv0_[kv][:, 0:1], [f"ssv0{kv}"], [f"ssv{kv}"])
                act(ssw_[kv][:, 0:32], ssv_[kv][:, 0:32], AF.Ln, [f"ssv{kv}"], [f"ssw{kv}"], scale=1.0 / 256, bias=EPS)
                act(ssw_[kv][:, 0:32], ssw_[kv][:, 0:32], AF.Exp, [f"ssw{kv}"], [f"ssw{kv}"], scale=-0.5)
                P.hard = False
                stt(vnb_[kv][:], z_[kv][:, 256:512], ssw_[kv][:, 0:1], vgB[:], ALU.mult, ALU.mult, [f"z{kv}", f"ssw{kv}", "vgB"], [f"vnb{kv}"])

            def stage_b(t4):
                tsl = slice(t4 * 128, (t4 + 1) * 128)
                kv = (nv0 + t4) % 2
                for gg in range(4):
                    mm(ps_va[:, 256 + gg * 64:256 + (gg + 1) * 64], wsr[:, gg, :], vnb_[kv][:, gg * 64:(gg + 1) * 64], True, True, ["wsr", f"vnb{kv}"], ["ps_va"])
                for gg in range(4):
                    stt(ob_[kv][:, gg * 64:(gg + 1) * 64], ps_va[:, 256 + gg * 64:256 + (gg + 1) * 64], vT[:, 69 + gg:70 + gg], z_[kv][:, gg * 64:(gg + 1) * 64],
                        ALU.add, ALU.mult, ["ps_va", f"z{kv}", "vT"], [f"ob{kv}"])
                for j in range(2):
                    tr(ps_tb[:, j * 128:(j + 1) * 128], ob_[kv][:, j * 128:(j + 1) * 128], identb[:], [f"ob{kv}", "identb"], ["ps_tb"])
                cp("act", obT[:, :, tsl], ps_tb[:].rearrange("p (j n) -> p j n", j=2), ["ps_tb"], ["obT"])

            ntt = W // 128
            stage_a(0)
            for t4 in range(ntt):
                if t4 + 1 < ntt:
                    stage_a(t4 + 1)
                if gate_on:
                    stage_b(t4)
            nv = nv0 + ntt
            if not (isctx and last):
                dma(mixT[256:512, off:off + W].rearrange("(j p) n -> p j n", p=128), obT[:, :, 0:W], ["obT"], [("mixT", g)], "d_obT")
            if gi >= 1:
                gather(gi - 1)
        if debug and l == 0:
            dma(dbg_k, kT_all[0], [("kT_all", 0)], [], "d_dbg")
            dma(dbg_v, V_all[0], [("V_all", 0)], [], "d_dbg")
        P.end()
        if stop_after == f"P{l}":
            P.close()
            return nc

        kb = P.sbuf("kb", [128, (NB + 2) * 128], BF16)
        vb = P.sbuf("vb", [128, NB + 2, 128], BF16)
        tbl = [P.sbuf(f"tbl{i}", [128, NSET * 8, 512], F32) for i in range(2)]
        qa = [P.sbuf(f"qa{i}", [128, 512], BF16) for i in range(2)]
        sb = [P.sbuf(f"sbias{i}", [128, 512], F32) for i in range(4)]
        pt = [P.sbuf(f"pt{i}", [128, 512], BF16) for i in range(4)]
        rcp = P.sbuf("rcp", [128, 512], F32)
        oa = [P.sbuf(f"oa{i}", [128, 512], BF16) for i in range(2)]
        ps_s = [P.psum(f"ps_s{i}", [128, 512]) for i in range(4)]
        ps_o = [P.psum(f"ps_o{i}", [128, 512]) for i in range(2)]
        ps_d = [P.psum(f"ps_d{i}", [128, 512]) for i in range(2)]
        nql = 0
        nt = 0
        for t in range(2):
            dma(kb[:, 0:256], kT_all[NG - 1][t * 128:(t + 1) * 128, 256:512], [], ["kb"], "d_kb")
            for g2 in range(NG):
                dma(kb[:, 256 + g2 * 512:256 + (g2 + 1) * 512], kT_in[g2][t * 128:(t + 1) * 128, :], [], ["kb"], "d_kb")
            dma(kb[:, 256 + NT:512 + NT], kT_all[0][768 + t * 128:768 + (t + 1) * 128, 0:256], [], ["kb"], "d_kb")
            dma(kb[:, NB * 128:(NB + 2) * 128], kTc[t * 128:(t + 1) * 128, :], [], ["kb"], "d_kb")
            dma(vb[:, 0:2, :], V_all[NG - 1][256:512, t * 128:(t + 1) * 128].rearrange("(k p) c -> p k c", p=128), [], ["vb"], "d_vb")
            for g2 in range(NG):
                dma(vb[:, 2 + g2 * 4:2 + (g2 + 1) * 4, :], V_in[g2][:, t * 128:(t + 1) * 128].rearrange("(k p) c -> p k c", p=128), [], ["vb"], "d_vb")
            dma(vb[:, 2 + NTT:NB, :], V_all[0][512:768, t * 128:(t + 1) * 128].rearrange("(k p) c -> p k c", p=128), [], ["vb"], "d_vb")
            dma(vb[:, NB:NB + 2, :], Vc[:, t * 128:(t + 1) * 128].rearrange("(k p) c -> p k c", p=128), [], ["vb"], "d_vb")
            for hh in range(2):
                dma(tbl[hh][:], nab[l, 2 * t + hh].rearrange("v p n -> p v n"), [], [f"tbl{hh}"], f"d_tbl{hh}")
            for (g, W, off, isctx) in act_groups:
                qab = qa[nql % 2]
                qak = f"qa{nql % 2}"
                dma(qab[:, 0:W], qT[t * 128:(t + 1) * 128, off:off + W], [("qT", g)], [qak], f"d_qa{nql % 2}")
                pso, psd = ps_o[nql % 2], ps_d[nql % 2]
                pok, pdk = f"ps_o{nql % 2}", f"ps_d{nql % 2}"
                for hh in range(2):
                    pl = slice(hh * 64, (hh + 1) * 64)
                    if isctx:
                        tiles = [(NB, None), (NB + 1, None)]
                    else:
                        vset = 0 if NSET == 1 else (0 if g == 0 else (2 if g == NG - 1 else 1))
                        tiles = [(g * 4 + i, vset * 8 + i) for i in range(8)] + [(NB, None), (NB + 1, None)]
                    base = nt

                    def smm_na(idx):
                        bt, var = tiles[idx]
                        k = (base + idx) % 4
                        mm(ps_s[k][:, 0:W], kb[pl, bt * 128:(bt + 1) * 128], qab[pl, 0:W], True, True, ["kb", qak], [f"ps_s{k}"])

                    for j0 in range(min(2, len(tiles))):
                        smm_na(j0)
                    for idx, (bt, var) in enumerate(tiles):
                        if idx + 2 < len(tiles):
                            smm_na(idx + 2)
                        k = (base + idx) % 4
                        pss, psk, ptb, ptk = ps_s[k], f"ps_s{k}", pt[k], f"pt{k}"
                        if var is None:
                            act(ptb[:, 0:W], pss[:, 0:W], AF.Exp, [psk], [ptk], scale=0.125)
                        else:
                            sbb = sb[k]
                            stt(sbb[:, 0:W], pss[:, 0:W], 0.125, tbl[hh][:, var, 0:W], ALU.mult, ALU.add, [psk, f"tbl{hh}"], [f"sbias{k}"])
                            act(ptb[:, 0:W], sbb[:, 0:W], AF.Exp, [f"sbias{k}"], [ptk])
                        first, lastt = idx == 0, idx == len(tiles) - 1
                        mm(pso[pl, 0:W], vb[:, bt, hh * 64:(hh + 1) * 64], ptb[:, 0:W], first, lastt, ["vb", ptk], [pok])
                        mm(psd[pl, 0:W], onesb[:, 0:64], ptb[:, 0:W], first, lastt, ["onesb", ptk], [pdk])
                        nt += 1
                A("dve", lambda e, psd=psd, W=W: e.reciprocal(out=rcp[:, 0:W], in_=psd[:, 0:W]), [pdk], ["rcp"])
                oab = oa[nql % 2]
                tt("dve", oab[:, 0:W], pso[:, 0:W], rcp[:, 0:W], ALU.mult, [pok, "rcp"], [f"oa{nql % 2}"])
                dma(mixT[t * 128:(t + 1) * 128, off:off + W], oab[:, 0:W], [f"oa{nql % 2}"], [("mixT", g)], f"d_oa{nql % 2}")
                nql += 1
        P.end()
        if stop_after == f"NA{l}":
            P.close()
            return nc

        kd = [P.sbuf(f"kd{i}", [128, (KT + 2) * 128], BF16) for i in range(2)]
        vd = [P.sbuf(f"vd{i}", [128, KT + 2, 128], BF16) for i in range(2)]
        qd = [P.sbuf(f"qd{i}", [128, 512], BF16) for i in range(2)]
        ptd = [P.sbuf(f"ptd{i}", [128, 2, 512], BF16) for i in range(3)]
        r12 = P.sbuf("r12", [128, 2, 512], F32)
        o1 = P.sbuf("o1", [128, 512], F32)
        o2 = P.sbuf("o2", [128, 512], F32)
        osq = P.sbuf("osq", [128, 512], BF16)
        lno = P.sbuf("lno", [128, 512], F32)
        oc = [P.sbuf(f"oc{i}", [128, 512], BF16) for i in range(2)]
        ps_sd = [P.psum(f"ps_sd{i}", [128, 2, 512]) for i in range(2)]
        ps_od = [P.psum(f"ps_od{m}", [128, 512]) for m in range(2)]
        ps_dd = P.psum("ps_dd", [128, 2, 512])
        dacc = P.sbuf("dacc", [128, 2, 512], F32)

        def load_head(h):
            b = h % 2
            for r in range(2):
                for g2 in range(NG):
                    dma(kd[b][:, (r * NG + g2) * 512:(r * NG + g2 + 1) * 512], kT_all[g2][r * 768 + (2 + h) * 128: r * 768 + (3 + h) * 128, :], [], [f"kd{b}"], f"d_kd{b}")
            dma(kd[b][:, KT * 128:(KT + 2) * 128], kTc[(2 + h) * 128:(3 + h) * 128, :], [], [f"kd{b}"], f"d_kd{b}")
            for r in range(2):
                for g2 in range(NG):
                    dma(vd[b][:, (r * NG + g2) * 4:(r * NG + g2 + 1) * 4, :], V_all[g2][r * 512:(r + 1) * 512, 256 + h * 128:256 + (h + 1) * 128].rearrange("(k p) c -> p k c", p=128),
                        [], [f"vd{b}"], f"d_vd{b}")
            dma(vd[b][:, KT:KT + 2, :], Vc[:, 256 + h * 128:256 + (h + 1) * 128].rearrange("(k p) c -> p k c", p=128), [], [f"vd{b}"], f"d_vd{b}")

        load_head(0)
        nql = 0
        for h in range(4):
            if h + 1 < 4:
                load_head(h + 1)
            b = h % 2
            kdb, vdb = kd[b], vd[b]
            for (g, W, off, isctx) in act_groups:
                qdb = qd[nql % 2]
                qdk = f"qd{nql % 2}"
                dma(qdb[:, 0:W], qT[(2 + h) * 128:(3 + h) * 128, off:off + W], [("qT", g)], [qdk], f"d_qd{nql % 2}")
                ktiles = [KT, KT + 1] if isctx else list(range(KT + 2))

                def smm(idx):
                    kt = ktiles[idx]
                    for m in range(2):
                        pl = slice(m * 64, (m + 1) * 64)
                        mm(ps_sd[idx % 2][:, m, 0:W], kdb[pl, kt * 128:(kt + 1) * 128], qdb[pl, 0:W], True, True, [f"kd{b}", qdk], [f"ps_sd{idx % 2}"])

                smm(0)
                for idx, kt in enumerate(ktiles):
                    if idx + 1 < len(ktiles):
                        smm(idx + 1)
                    first, lastt = idx == 0, idx == len(ktiles) - 1
                    ptb = ptd[idx % 3]
                    ptk = f"ptd{idx % 3}"
                    act(ptb[:, :, 0:W], ps_sd[idx % 2][:, :, 0:W], AF.Exp, [f"ps_sd{idx % 2}"], [ptk], scale=0.125)
                    for m in range(2):
                        mm(ps_od[m][:, 0:W], vdb[:, kt, :], ptb[:, m, 0:W], first, lastt, [f"vd{b}", ptk], [f"ps_od{m}"])
                    if first:
                        cp("dve", ps_dd[:, :, 0:W], ptb[:, :, 0:W], [ptk], ["ps_dd"])
                    else:
                        tt("dve", ps_dd[:, :, 0:W], ps_dd[:, :, 0:W], ptb[:, :, 0:W], ALU.add, [ptk, "ps_dd"], ["ps_dd"])
                cp("act", dacc[:, :, 0:W], ps_dd[:, :, 0:W], ["ps_dd"], ["dacc"])
                for m in range(2):
                    mm(ps_dd[:, m, 0:W], onesf[:], dacc[:, m, 0:W], True, True, ["onesf", "dacc"], ["ps_dd"])
                A("dve", lambda e, W=W: e.reciprocal(out=r12[:, :, 0:W], in_=ps_dd[:, :, 0:W]), ["ps_dd"], ["r12"])
                tt("dve", o1[:, 0:W], ps_od[0][:, 0:W], r12[:, 0, 0:W], ALU.mult, ["ps_od0", "r12"], ["o1"])
                tt("dve", o2[:, 0:W], ps_od[1][:, 0:W], r12[:, 1, 0:W], ALU.mult, ["ps_od1", "r12"], ["o2"])
                stt(o1[:, 0:W], o2[:, 0:W], nlam[:, 0:1], o1[:, 0:W], ALU.mult, ALU.add, ["o1", "o2", "nlam"], ["o1"])
                tt("pool", osq[:, 0:W], o1[:, 0:W], o1[:, 0:W], ALU.mult, ["o1"], ["osq"])
                pn = ps_sd[0]
                mm(pn[:, 0, 0:W], onesb[:], osq[:, 0:W], True, True, ["osq", "onesb"], ["ps_sd0"])
                act(lno[:, 0:W], pn[:, 0, 0:W], AF.Ln, ["ps_sd0"], ["lno"], scale=1.0 / 128, bias=EPS)
                act(lno[:, 0:W], lno[:, 0:W], AF.Exp, ["lno"], ["lno"], scale=-0.5)
                ocb = oc[nql % 2]
                stt(ocb[:, 0:W], o1[:, 0:W], sgl[:, 0:1], lno[:, 0:W], ALU.mult, ALU.mult, ["o1", "lno", "sgl"], [f"oc{nql % 2}"])
                dma(mixT[512 + h * 128:512 + (h + 1) * 128, off:off + W], ocb[:, 0:W], [f"oc{nql % 2}"], [("mixT", g)], f"d_oc{nql % 2}")
                nql += 1
        P.end()
        if stop_after == f"DA{l}":
            P.close()
            return nc

        wo = P.sbuf("wo", [128, 8, D], BF16)
        for c in range(8):
            dma(wo[:, c, :], w_out[l][c * 128:(c + 1) * 128, :], [], ["wo"], "d_wo", eng="pool")
        mx = [P.sbuf(f"mx{i}", [128, 8, 512], BF16) for i in range(2)]
        xg_o = [P.sbuf(f"xgo{i}", [128, 8, 512], F32) for i in range(2)]
        sq = P.sbuf("sq", [128, 8, 512], BF16)
        rs = P.sbuf("rs", [128, 512], F32)
        tmpb = [P.sbuf(f"tmpb{i}", [128, 512], F32) for i in range(2)]
        h2 = [P.sbuf(f"h2{i}", [128, 8, 512], BF16) for i in range(2)]
        ps_y = [P.psum(f"ps_y{i}", [128, 512]) for i in range(2)]
        ps_st = P.psum("ps_st", [128, 512])
        rst = P.sbuf("rst", [128, 512], F32)

        def load_o1(gi):
            g, W, off, isctx = act_groups[gi]
            b = gi % 2
            dma(mx[b][:, :, 0:W], mixT[:, off:off + W].rearrange("(c p) n -> p c n", p=128), [("mixT", g)], [f"mx{b}"], f"d_mx{b}")
            dma(xg_o[b][:, :, 0:W], xT[:, off:off + W].rearrange("(c p) n -> p c n", p=128), [("xT", g)], [f"xgo{b}"], f"d_xgo{b}")

        load_o1(0)
        for gi, (g, W, off, isctx) in enumerate(act_groups):
            if gi + 1 < len(act_groups):
                load_o1(gi + 1)
            b = gi % 2
            s = 1 if isctx else 0
            xg = xg_o[b]
            for c in range(8):
                py = ps_y[c % 2]
                for kc in range(8):
                    mm(py[:, 0:W], wo[:, kc, c * 128:(c + 1) * 128], mx[b][:, kc, 0:W], kc == 0, kc == 7, ["wo", f"mx{b}"], [f"ps_y{c % 2}"])
                stt(xg[:, c, 0:W], py[:, 0:W], Bv(2, c, s), xg[:, c, 0:W], ALU.mult, ALU.add, [f"ps_y{c % 2}", f"xgo{b}", "modT"], [f"xgo{b}"])
            dma(xT[:, off:off + W].rearrange("(c p) n -> p c n", p=128), xg[:, :, 0:W], [f"xgo{b}"], [("xT", g)], f"d_xs{b}")
            hb = h2[b]
            norm_mod(xg, W, A2, 3, s, hb, sq, rs, tmpb, ps_st, f"xgo{b}", "ps_st", rst)
            dma(h2T[:, off:off + W].rearrange("(c p) n -> p c n", p=128), hb[:, :, 0:W], ["hT"], [("h2T", g)], f"d_h2{b}")
        P.end()
        if stop_after == f"O1{l}":
            P.close()
            return nc

        w1s = P.sbuf("w1s", [128, 8, DFF], BF16)
        w3s = P.sbuf("w3s", [128, 8, DFF], BF16)
        for c in range(8):
            for hh in range(2):
                dma(w1s[:, c, hh * 1408:(hh + 1) * 1408], w1[l][c * 128:(c + 1) * 128, hh * 1408:(hh + 1) * 1408], [], ["w1s"], "d_w1", eng="pool")
                dma(w3s[:, c, hh * 1408:(hh + 1) * 1408], w3[l][c * 128:(c + 1) * 128, hh * 1408:(hh + 1) * 1408], [], ["w3s"], "d_w3", eng="pool")
        h2i = [P.sbuf(f"h2i{i}", [128, 8, 512], BF16) for i in range(2)]
        sl = [P.sbuf(f"sl{i}", [128, 512], F32) for i in range(2)]
        gb = [P.sbuf(f"gb{i}", [128, 2, 512], BF16) for i in range(2)]
        ps_a = [P.psum(f"ps_a{i}", [128, 512]) for i in range(2)]
        ps_b = [P.psum(f"ps_b{i}", [128, 512]) for i in range(2)]

        def load_o2(gi):
            g, W, off, isctx = act_groups[gi]
            dma(h2i[gi % 2][:, :, 0:W], h2T[:, off:off + W].rearrange("(c p) n -> p c n", p=128), [("h2T", g)], [f"h2i{gi % 2}"], f"d_h2i{gi % 2}")

        load_o2(0)
        nj = 0
        for gi, (g, W, off, isctx) in enumerate(act_groups):
            if gi + 1 < len(act_groups):
                load_o2(gi + 1)
            hb = h2i[gi % 2]
            hk = f"h2i{gi % 2}"
            for j in range(DFF // 128):
                pa, pb = ps_a[j % 2], ps_b[j % 2]
                for kc in range(8):
                    mm(pa[:, 0:W], w1s[:, kc, j * 128:(j + 1) * 128], hb[:, kc, 0:W], kc == 0, kc == 7, ["w1s", hk], [f"ps_a{j % 2}"])
                for kc in range(8):
                    mm(pb[:, 0:W], w3s[:, kc, j * 128:(j + 1) * 128], hb[:, kc, 0:W], kc == 0, kc == 7, ["w3s", hk], [f"ps_b{j % 2}"])
                slb = sl[j % 2]
                act(slb[:, 0:W], pa[:, 0:W], AF.Silu, [f"ps_a{j % 2}"], [f"sl{j % 2}"])
                gbb = gb[(nj // 2) % 2]
                tt("dve", gbb[:, j % 2, 0:W], slb[:, 0:W], pb[:, 0:W], ALU.mult, [f"sl{j % 2}", f"ps_b{j % 2}"], [f"gb{(nj // 2) % 2}"])
                if j % 2 == 1:
                    dma(gT[(j - 1) * 128:(j + 1) * 128, off:off + W].rearrange("(c p) n -> p c n", p=128), gbb[:, :, 0:W], [f"gb{(nj // 2) % 2}"], [("gT", g)],
                        f"d_gb{(nj // 2) % 2}")
                nj += 1
        P.end()
        if stop_after == f"O2{l}":
            P.close()
            return nc

        w2s = P.sbuf("w2s", [128, DFF // 128, D], BF16)
        for j in range(DFF // 128):
            dma(w2s[:, j, :], w2[l][j * 128:(j + 1) * 128, :], [], ["w2s"], "d_w2", eng="pool")
        gi_b = [P.sbuf(f"gi{i}", [128, DFF // 128, 512], BF16) for i in range(2)]
        xg_f = [P.sbuf(f"xgf{i}", [128, 8, 512], F32) for i in range(2)]
        ot = [P.sbuf(f"ot{i}", [128, D], F32) for i in range(2)]
        ps_y = [P.psum(f"ps_y{i}", [128, 512]) for i in range(2)]
        ps_t = [P.psum(f"ps_t{i}", [128, 512]) for i in range(4)]

        def load_o3(gi):
            g, W, off, isctx = act_groups[gi]
            b = gi % 2
            for hh in range(2):
                dma(gi_b[b][:, hh * 11:(hh + 1) * 11, 0:W], gT[hh * 1408:(hh + 1) * 1408, off:off + W].rearrange("(c p) n -> p c n", p=128), [("gT", g)], [f"gi{b}"], f"d_gi{b}")
            dma(xg_f[b][:, :, 0:W], xT[:, off:off + W].rearrange("(c p) n -> p c n", p=128), [("xT", g)], [f"xgf{b}"], f"d_xgf{b}")

        load_o3(0)
        n = 0
        for gi, (g, W, off, isctx) in enumerate(act_groups):
            if gi + 1 < len(act_groups):
                load_o3(gi + 1)
            b = gi % 2
            s = 1 if isctx else 0
            xg = xg_f[b]
            for c in range(8):
                py = ps_y[c % 2]
                for j in range(DFF // 128):
                    mm(py[:, 0:W], w2s[:, j, c * 128:(c + 1) * 128], gi_b[b][:, j, 0:W], j == 0, j == DFF // 128 - 1, ["w2s", f"gi{b}"], [f"ps_y{c % 2}"])
                stt(xg[:, c, 0:W], py[:, 0:W], Bv(5, c, s), xg[:, c, 0:W], ALU.mult, ALU.add, [f"ps_y{c % 2}", f"xgf{b}", "modT"], [f"xgf{b}"])
            if not last:
                dma(xT[:, off:off + W].rearrange("(c p) n -> p c n", p=128), xg[:, :, 0:W], [f"xgf{b}"], [("xT", g)], f"d_xs{b}")
            else:
                for t4 in range(W // 128):
                    otb = ot[n % 2]
                    for bb in range(2):
                        pb = ps_t[(n % 2) * 2 + bb]
                        for c4 in range(4):
                            c = bb * 4 + c4
                            tr(pb[:, c4 * 128:(c4 + 1) * 128], xg[:, c, t4 * 128:(t4 + 1) * 128], identf[:], [f"xgf{b}", "identf"], [f"ps_t{(n % 2) * 2 + bb}"])
                        cp("act" if bb == 0 else "dve", otb[:, bb * 512:(bb + 1) * 512], pb[:], [f"ps_t{(n % 2) * 2 + bb}"], [f"ot{n % 2}"])
                    dma(out[off + t4 * 128: off + (t4 + 1) * 128, :], otb[:], [f"ot{n % 2}"], [], f"d_ot{n % 2}")
                    n += 1
        P.end()
    P.close()
    return nc


def _rope_tables(S, half, NT):
    t = np.arange(half * NT, (half + 1) * NT, dtype=np.int32)
    row = (t // 64).astype(np.float32)
    col = (t % 64).astype(np.float32)
    inv = (np.float32(10000.0) ** (-np.arange(0, 32, 2, dtype=np.float32) / np.float32(32))).astype(np.float32)
    ar = row[:, None] * inv[None, :]
    ac = col[:, None] * inv[None, :]
    ang = np.concatenate([ar, ar, ac, ac], axis=-1)
    cos = np.cos(ang).astype(np.float32)
    sin = np.sin(ang).astype(np.float32)
    sign = np.ones(64, np.float32)
    for a in range(2):
        sign[a * 32:a * 32 + 16] = -1.0
    sin = sin * sign[None, :]
    cosT = np.ascontiguousarray(np.concatenate([cos.T, cos.T], axis=0))
    sinT = np.ascontiguousarray(np.concatenate([sin.T, sin.T], axis=0))
    return cosT, sinT


def _rperm():
    m = np.zeros((128, 128), np.float32)
    for base in (0, 64):
        for dp in range(64):
            tq = (dp % 32) // 16
            src = dp + 16 if tq == 0 else dp - 16
            m[base + src, base + dp] = 1.0
    return m


def _na_bias_tables(rpb, S, half, NT, NG):
    rows = S // 64
    NSET = 1 if NG == 1 else 3
    out = np.full((4, NSET * 8, 128, 512), -1e30, np.float32)
    kh = 8
    r_all = np.arange(rows)
    kr0 = np.clip(r_all - kh // 2, 0, rows - kh)
    cq = np.arange(64)
    c0 = np.clip(cq - 8, 0, 64 - 16)
    sets = [0] if NG == 1 else [0, 1, 2]
    for si, sset in enumerate(sets):
        if NG == 1:
            g = 0
        else:
            g = 0 if sset == 0 else (NG - 1 if sset == 2 else 1)
        R = half * (rows // 2) + g * 8
        qrow = R + np.arange(512) // 64
        qcol = np.arange(512) % 64
        for i in range(8):
            bt = g * 4 + i
            kt = half * (NT // 128) + bt - 2
            if kt < 0 or kt >= S // 128:
                continue
            krow = kt * 2 + np.arange(128) // 64
            kcol = np.arange(128) % 64
            rv = (krow[:, None] >= kr0[qrow][None, :]) & (krow[:, None] < kr0[qrow][None, :] + kh)
            cvd = (kcol[:, None] >= c0[qcol][None, :]) & (kcol[:, None] < c0[qcol][None, :] + 16)
            roff = np.clip(krow[:, None] - qrow[None, :] + 7, 0, 14)
            coff = np.clip(kcol[:, None] - qcol[None, :], -15, 15) + 15
            valid = rv & cvd
            for h in range(4):
                b = rpb[h][roff, coff]
                out[h, si * 8 + i] = np.where(valid, b, np.float32(-1e30))
    return out


def prepare_inputs(inp, S):
    NT = S // 2
    NG = NT // 512
    L = inp["w_mod"].shape[0]
    f = lambda a: np.ascontiguousarray(np.asarray(a, dtype=np.float32))
    x, c, ctx, c_ctx = f(inp["x"]), f(inp["c"]), f(inp["ctx"]), f(inp["c_ctx"])
    vecs = np.zeros((L, 128, 128), np.float32)
    for l in range(L):
        vecs[l, 0:48] = f(inp["b_mod"])[l].reshape(48, 128)
        vecs[l, 48:56] = f(inp["norm1_g"])[l].reshape(8, 128)
        vecs[l, 56:64] = f(inp["norm2_g"])[l].reshape(8, 128)
        vecs[l, 64] = np.tile(f(inp["na_q_g"])[l], 2)
        vecs[l, 65] = np.tile(f(inp["na_k_g"])[l], 2)
        vecs[l, 66] = np.tile(f(inp["da_q_g"])[l], 2)
        vecs[l, 67] = np.tile(f(inp["da_k_g"])[l], 2)
        vecs[l, 68] = f(inp["da_sub_g"])[l]
        vecs[l, 69:73] = f(inp["gm_bs"])[l]
    lamv = np.stack([np.concatenate([f(inp[k])[l] for k in ("da_lq1", "da_lk1", "da_lq2", "da_lk2")]) for l in range(L)])[:, None, :]
    vg = f(inp["gm_v_g"])[:, None, :]
    wsT = np.ascontiguousarray(np.transpose(f(inp["gm_ws"]), (0, 1, 3, 2)))
    common = dict(vecs=vecs, lamv=np.ascontiguousarray(lamv), vg=np.ascontiguousarray(vg), w_mod=f(inp["w_mod"]), w_in=f(inp["w_in"]),
                  w_out=f(inp["w_out"]), w1=f(inp["ffn_w1"]), w3=f(inp["ffn_w3"]), w2=f(inp["ffn_w2"]), wsT=wsT,
                  ident=np.eye(128, dtype=np.float32), rperm=_rperm(),
                  blk=np.kron(np.eye(2, dtype=np.float32), np.ones((64, 64), np.float32)))
    rpb = f(inp["na_rpb"])
    maps = []
    tabs = {}
    for core in range(8):
        b, half = core // 2, core % 2
        if half not in tabs:
            cosT, sinT = _rope_tables(S, half, NT)
            nabt = np.stack([_na_bias_tables(rpb[l], S, half, NT, NG) for l in range(L)])
            tabs[half] = (cosT, sinT, nabt)
        cosT, sinT, nabt = tabs[half]
        m = dict(common)
        m["x"] = np.ascontiguousarray(x[b, half * NT:(half + 1) * NT])
        m["ctx"] = np.ascontiguousarray(ctx[b])
        m["cvec"] = np.ascontiguousarray(np.concatenate([c[b].reshape(8, 128), c_ctx.reshape(8, 128)], axis=0))
        m["cos"], m["sin"], m["nab"] = cosT, sinT, nabt
        maps.append(m)
    return maps


_CACHE = {}


def kernel(**inputs):
    S = int(np.asarray(inputs["x"]).shape[1])
    B = int(np.asarray(inputs["x"]).shape[0])
    assert B == 4
    if S not in _CACHE:
        _CACHE[S] = build(S)
    nc = _CACHE[S]
    maps = prepare_inputs(inputs, S)
    res = run_bass_kernel_spmd(nc, maps, core_ids=list(range(8)))
    NT = S // 2
    out = np.empty((B, S, D), np.float32)
    for core in range(8):
        b, half = core // 2, core % 2
        out[b, half * NT:(half + 1) * NT] = res.results[core]["out"]
    return out
```

```python
import contextlib
import math
import numpy as np
import concourse.bass as bass
import concourse.mybir as mybir
from concourse.bass_utils import run_bass_kernel_spmd

F32 = mybir.dt.float32
BF16 = mybir.dt.bfloat16
AF = mybir.ActivationFunctionType
ALU = mybir.AluOpType

ENGS = ("pe", "act", "dve", "pool", "sp")
D = 1024
CTX = 256
INW = 2816
DFF = 2816
EPS = 1e-6
PAIRS = [[0, 1], [2, 3], [4, 5], [6, 7]]


class Op:
    __slots__ = ("eng", "fn", "waits", "dma", "dsem", "signal", "count", "dinc")

    def __init__(self, eng, fn, dma=False):
        self.eng = eng
        self.fn = fn
        self.waits = []
        self.dma = dma
        self.dsem = None
        self.signal = False
        self.count = 0
        self.dinc = 16


class Prog:
    def __init__(self, nc):
        self.nc = nc
        self.outer = contextlib.ExitStack()
        self.sems = {}
        self.dcum = {}
        self.cnt = {e: 0 for e in ENGS}
        self.waited = {e: {} for e in ENGS}
        self.stack = None
        self.nalloc = 0
        self.hard = False
        self.begin()

    def begin(self):
        self.ops = {e: [] for e in ENGS}
        self.track = {}
        self.stack = contextlib.ExitStack()

    def sbuf(self, name, shape, dtype, outer=False):
        st = self.outer if outer else self.stack
        self.nalloc += 1
        return st.enter_context(self.nc.sbuf_tensor(f"{name}_s{self.nalloc}", list(shape), dtype))

    def psum(self, name, shape, dtype=F32):
        self.nalloc += 1
        return self.stack.enter_context(self.nc.psum_tensor(f"{name}_p{self.nalloc}", list(shape), dtype))

    def sem(self, name):
        if name not in self.sems:
            self.sems[name] = self.outer.enter_context(self.nc.semaphore(name))
        return self.sems[name]

    def _dep(self, op, prod):
        if prod is None or prod is op:
            return
        if prod.dma:
            op.waits.append((prod.dsem, self.dcum[prod.dsem]))
        else:
            if prod.eng == op.eng and (not self.hard or op.eng == "pe"):
                return
            prod.signal = True
            op.waits.append(prod)

    def add(self, eng, fn, reads=(), writes=(), dsem=None, dinc=16):
        dma = dsem is not None
        op = Op(eng, fn, dma)
        for k in reads:
            t = self.track.get(k)
            if t is not None:
                self._dep(op, t[0])
        for k in writes:
            t = self.track.get(k)
            if t is not None:
                self._dep(op, t[0])
                for r in t[1].values():
                    self._dep(op, r)
        if dma:
            op.dsem = dsem
            self.dcum[dsem] = self.dcum.get(dsem, 0) + dinc
            op.dinc = dinc
        for k in reads:
            t = self.track.setdefault(k, [None, {}])
            t[1][eng if not dma else ("dma", dsem)] = op
        for k in writes:
            self.track[k] = [op, {}]
        self.ops[eng].append(op)
        return op

    def end(self, barrier=True):
        nc = self.nc
        for e in ENGS:
            for op in self.ops[e]:
                if op.signal and not op.dma:
                    self.cnt[e] += 1
                    op.count = self.cnt[e]
        esem = {e: self.sem("s_" + e) for e in ENGS}
        for d in self.dcum:
            self.sem(d)
        sems = self.sems

        def run(en):
            def body(eng):
                waited = self.waited[en]
                for op in self.ops[en]:
                    for w in op.waits:
                        if isinstance(w, tuple):
                            sname, val = w
                        else:
                            sname, val = "s_" + w.eng, w.count
                        if waited.get(sname, 0) >= val:
                            continue
                        waited[sname] = val
                        eng.wait_ge(sems[sname], val)
                    ins = op.fn(eng)
                    if op.dma:
                        ins.then_inc(sems[op.dsem], op.dinc)
                    elif op.signal:
                        ins.then_inc(esem[en], 1)
                if en == "sp":
                    for d, c in self.dcum.items():
                        if waited.get(d, 0) < c:
                            waited[d] = c
                            eng.wait_ge(sems[d], c)
            return body

        with nc.Block() as block:
            block.tensor(run("pe"))
            block.scalar(run("act"))
            block.vector(run("dve"))
            block.gpsimd(run("pool"))
            block.sync(run("sp"))
        if barrier:
            nc.all_engine_barrier()
        self.stack.close()
        self.begin()

    def close(self):
        self.stack.close()
        self.outer.close()


def build(S, nlayers=2, debug=False, stop_after=None, gpad=None):
    NT = S // 2
    NG = NT // 512
    NTX = NT + CTX
    KT = S // 128
    NTT = NT // 128
    NB = NTT + 4
    NSET = 1 if NG == 1 else 3
    nc = bass.Bass("TRN2", target_bir_lowering=False)

    def din(name, shape, dt=F32):
        return nc.dram_tensor(name, list(shape), dt, kind="ExternalInput").ap()

    x_in = din("x", [NT, D])
    ctx_in = din("ctx", [CTX, D])
    cvec = din("cvec", [16, 128])
    vecs = din("vecs", [nlayers, 128, 128])
    lamv = din("lamv", [nlayers, 1, 256])
    vg_in = din("vg", [nlayers, 1, 256])
    w_mod = din("w_mod", [nlayers, D, 6 * D])
    w_in = din("w_in", [nlayers, D, INW])
    w_out = din("w_out", [nlayers, D, D])
    w1 = din("w1", [nlayers, D, DFF])
    w3 = din("w3", [nlayers, D, DFF])
    w2 = din("w2", [nlayers, DFF, D])
    wsT_in = din("wsT", [nlayers, 4, 128, 128])
    cos_in = din("cos", [128, NT])
    sin_in = din("sin", [128, NT])
    nab = din("nab", [nlayers, 4, NSET * 8, 128, 512])
    ident_in = din("ident", [128, 128])
    rperm_in = din("rperm", [128, 128])
    blk_in = din("blk", [128, 128])
    out = nc.dram_tensor("out", [NT, D], F32, kind="ExternalOutput").ap()

    def scratch(name, shape, dt):
        if debug:
            return nc.dram_tensor(name, list(shape), dt, kind="ExternalOutput")
        return nc.dram_tensor(name, list(shape), dt)

    xT = scratch("xT", [D, NTX], F32).ap()
    qT = scratch("qT", [768, NTX], BF16).ap()
    kT_in_t = [nc.dram_tensor(f"kT_in{g}", [768, 512], BF16) for g in range(NG)]
    kT_all_t = [nc.dram_tensor(f"kT_all{g}", [2 * 768, 512], BF16) for g in range(NG)]
    V_in_t = [nc.dram_tensor(f"V_in{g}", [512, 768], BF16) for g in range(NG)]
    V_all_t = [nc.dram_tensor(f"V_all{g}", [2 * 512, 768], BF16) for g in range(NG)]
    kT_in = [t.ap() for t in kT_in_t]
    kT_all = [t.ap() for t in kT_all_t]
    V_in = [t.ap() for t in V_in_t]
    V_all = [t.ap() for t in V_all_t]
    kTc = scratch("kTc", [768, CTX], BF16).ap()
    Vc = scratch("Vc", [CTX, 768], BF16).ap()
    mixT = scratch("mixT", [D, NTX], BF16).ap()
    h2T = scratch("h2T", [D, NTX], BF16).ap()
    gT = scratch("gT", [DFF, NTX], BF16).ap()
    if debug:
        dbg_k = nc.dram_tensor("dbg_k", [2 * 768, 512], BF16, kind="ExternalOutput").ap()
        dbg_v = nc.dram_tensor("dbg_v", [2 * 512, 768], BF16, kind="ExternalOutput").ap()

    P = Prog(nc)
    A = P.add

    identf = P.sbuf("identf", [128, 128], F32, outer=True)
    identb = P.sbuf("identb", [128, 128], BF16, outer=True)
    rperm = P.sbuf("rperm", [128, 128], BF16, outer=True)
    blk = P.sbuf("blk", [128, 128], BF16, outer=True)
    onesb = P.sbuf("onesb", [128, 128], BF16, outer=True)
    onesf = P.sbuf("onesf", [128, 128], F32, outer=True)
    vT = P.sbuf("vT", [128, 128], F32, outer=True)
    modT = P.sbuf("modT", [128, 48, 2], F32, outer=True)
    A1 = P.sbuf("A1", [128, 8, 2], F32, outer=True)
    A2 = P.sbuf("A2", [128, 8, 2], F32, outer=True)
    sgl = P.sbuf("sgl", [128, 1], F32, outer=True)
    nlam = P.sbuf("nlam", [128, 1], F32, outer=True)
    vgB = P.sbuf("vgB", [128, 256], F32, outer=True)
    scT = P.sbuf("scT", [128, 8, 2], F32, outer=True)

    groups = [(g, 512, g * 512, False) for g in range(NG)] + [(NG, CTX, NT, True)]

    def mm(o, l, r, st, sp, rd, wr):
        A("pe", lambda e: e.matmul(o, lhsT=l, rhs=r, start=st, stop=sp), rd, wr)

    def tr(o, i, idn, rd, wr):
        A("pe", lambda e: e.transpose(out=o, in_=i, identity=idn), rd, wr)

    def act(o, i, f, rd, wr, scale=1.0, bias=0.0, eng="act"):
        A(eng, lambda e: e.activation(out=o, in_=i, func=f, scale=scale, bias=bias), rd, wr)

    def cp(eng, o, i, rd, wr):
        if eng == "act":
            A("act", lambda e: e.copy(out=o, in_=i), rd, wr)
        else:
            A(eng, lambda e: e.tensor_copy(out=o, in_=i), rd, wr)

    def tt(eng, o, a, b, op, rd, wr):
        A(eng, lambda e: e.tensor_tensor(out=o, in0=a, in1=b, op=op), rd, wr)

    def ts(eng, o, a, s1, s2, op0, op1, rd, wr):
        if s2 is None:
            A(eng, lambda e: e.tensor_scalar(out=o, in0=a, scalar1=s1, scalar2=None, op0=op0), rd, wr)
        else:
            A(eng, lambda e: e.tensor_scalar(out=o, in0=a, scalar1=s1, scalar2=s2, op0=op0, op1=op1), rd, wr)

    def stt(o, a, sc, b, op0, op1, rd, wr):
        A("dve", lambda e: e.scalar_tensor_tensor(out=o, in0=a, scalar=sc, in1=b, op0=op0, op1=op1), rd, wr)

    def dma(o, i, rd, wr, sem, eng="sp"):
        A(eng, lambda e: e.dma_start(out=o, in_=i), rd, wr, dsem=sem)

    def rstd_from_sum(o, ps, n, rd, wr, tmp):
        act(tmp, ps, AF.Ln, rd, [wr + "_t"], scale=1.0 / n, bias=EPS)
        act(o, tmp, AF.Exp, [wr + "_t"], [wr], scale=-0.5)

    P.hard = True
    st32 = P.sbuf("c_st", [128, 3, 128], F32)
    dma(identf[:], ident_in, [], ["identf"], "d_c0")
    dma(st32[:, 0, :], ident_in, [], ["st0"], "d_c1")
    dma(st32[:, 1, :], rperm_in, [], ["st1"], "d_c1")
    dma(st32[:, 2, :], blk_in, [], ["st2"], "d_c1")
    cp("dve", identb[:], st32[:, 0, :], ["st0"], ["identb"])
    cp("dve", rperm[:], st32[:, 1, :], ["st1"], ["rperm"])
    cp("dve", blk[:], st32[:, 2, :], ["st2"], ["blk"])
    A("dve", lambda e: e.memset(onesb[:], 1.0), [], ["onesb"])
    A("dve", lambda e: e.memset(onesf[:], 1.0), [], ["onesf"])
    cv = P.sbuf("cv", [16, 128], F32)
    ps_c = P.psum("ps_c", [128, 16])
    dma(cv[:], cvec, [], ["cv"], "d_c2")
    tr(ps_c[:, :], cv[:], identf[0:16, 0:16], ["cv", "identf"], ["ps_c"])
    act(scT[:].rearrange("p k s -> p s k"), ps_c[:].rearrange("p (s k) -> p s k", s=2), AF.Silu, ["ps_c"], ["scT"])
    P.end()
    P.hard = False

    xin = [P.sbuf(f"xin{i}", [128, D], F32) for i in range(2)]
    xg_t = [P.sbuf(f"xgT{i}", [128, 8, 512], F32) for i in range(2)]
    ps_t = [P.psum(f"ps_t{i}", [128, 512]) for i in range(4)]
    n = 0
    for (g, W, off, isctx) in groups:
        xb = xg_t[g % 2]
        for t4 in range(W // 128):
            src = ctx_in[t4 * 128:(t4 + 1) * 128, :] if isctx else x_in[off + t4 * 128: off + (t4 + 1) * 128, :]
            xi = xin[n % 2]
            dma(xi[:], src, [], [f"xin{n % 2}"], f"d_xin{n % 2}")
            for b in range(2):
                pb = ps_t[(n % 2) * 2 + b]
                for c4 in range(4):
                    c = b * 4 + c4
                    tr(pb[:, c4 * 128:(c4 + 1) * 128], xi[:, c * 128:(c + 1) * 128], identf[:], [f"xin{n % 2}", "identf"], [f"ps_t{(n % 2) * 2 + b}"])
                cp("act" if b == 0 else "dve", xb[:, b * 4:(b + 1) * 4, t4 * 128:(t4 + 1) * 128], pb[:].rearrange("p (c n) -> p c n", c=4),
                   [f"ps_t{(n % 2) * 2 + b}"], [f"xgT{g % 2}"])
            n += 1
        dma(xT[:, off:off + W].rearrange("(c p) n -> p c n", p=128), xb[:, :, 0:W], [f"xgT{g % 2}"], [("xT", g)], f"d_xgT{g % 2}")
    P.end()
    if stop_after == "T":
        P.close()
        return nc

    for l in range(nlayers):
        last = (l == nlayers - 1)
        lam_init = 0.8 - 0.6 * math.exp(-0.3 * l)
        act_groups = groups[:NG] if last else groups

        P.hard = True
        vraw = P.sbuf("vraw", [128, 128], F32)
        ps_v = P.psum("ps_v", [128, 128])
        dma(vraw[:], vecs[l], [], ["vraw"], "d_m0")
        tr(ps_v[:, :], vraw[:], identf[:], ["vraw"], ["ps_v"])
        cp("act", vT[:], ps_v[:], ["ps_v"], ["vT"])
        ts("dve", sgl[:], vT[:, 68:69], 1.0 - lam_init, None, ALU.mult, None, ["vT"], ["sgl"])
        lv = P.sbuf("lv", [1, 256], F32)
        lacc = P.sbuf("lacc", [1, 32], F32)
        lacc0 = P.sbuf("lacc0", [1, 4], F32)
        A("dve", lambda e: e.memset(lacc[:], 0.0), [], ["lacc"])
        ljunk = P.sbuf("ljunk", [1, 64], F32)
        ps_l = P.psum("ps_l", [128, 2])
        dma(lv[:], lamv[l], [], ["lv"], "d_m1")
        for i in range(2):
            tt("dve", ljunk[:], lv[:, (2 * i) * 64:(2 * i + 1) * 64], lv[:, (2 * i + 1) * 64:(2 * i + 2) * 64], ALU.mult, ["lv"], ["ljunk"])
            A("dve", lambda e, i=i: e.tensor_reduce(out=lacc0[:, i:i + 1], in_=ljunk[:], axis=mybir.AxisListType.X, op=ALU.add), ["ljunk"], ["lacc0"])
        cp("dve", lacc[:, 0:2], lacc0[:, 0:2], ["lacc0"], ["lacc"])
        act(lacc[:, 0:32], lacc[:, 0:32], AF.Exp, ["lacc"], ["lacc"])
        tt("dve", lacc[:, 2:3], lacc[:, 1:2], lacc[:, 0:1], ALU.subtract, ["lacc"], ["lacc"])
        ts("dve", lacc[:, 2:4], lacc[:, 2:4], 1.0, -lam_init, ALU.mult, ALU.add, ["lacc"], ["lacc"])
        mm(ps_l[:, 0:2], onesf[0:1, :], lacc[:, 2:4], True, True, ["lacc", "onesf"], ["ps_l"])
        cp("act", nlam[:], ps_l[:, 0:1], ["ps_l"], ["nlam"])
        vgr = P.sbuf("vgr", [1, 256], F32)
        ps_g = P.psum("ps_g", [128, 256])
        dma(vgr[:], vg_in[l], [], ["vgr"], "d_m2")
        mm(ps_g[:, :], onesf[0:1, :], vgr[:], True, True, ["vgr", "onesf"], ["ps_g"])
        cp("act", vgB[:], ps_g[:], ["ps_g"], ["vgB"])
        wm = [P.sbuf(f"wm{i}", [128, 8, 1024], F32) for i in range(2)]
        ps_m = P.psum("ps_m", [128, 96])
        for j in range(6):
            wb_ = wm[j % 2]
            for hh in range(2):
                dma(wb_[:, hh * 4:(hh + 1) * 4, :], w_mod[l][hh * 512:(hh + 1) * 512, j * 1024:(j + 1) * 1024].rearrange("(c p) n -> p c n", p=128),
                    [], [f"wm{j % 2}"], f"d_wm{j % 2}")
            for c in range(8):
                ct = j * 8 + c
                for kc in range(8):
                    mm(ps_m[:, ct * 2:ct * 2 + 2], wb_[:, kc, c * 128:(c + 1) * 128], scT[:, kc, :], kc == 0, kc == 7, [f"wm{j % 2}", "scT"], ["ps_m"])
        for s in range(2):
            tt("dve", modT[:, :, s], ps_m[:].rearrange("p (c s) -> p c s", s=2)[:, :, s], vT[:, 0:48], ALU.add, ["ps_m", "vT"], ["modT"])
        for s in range(2):
            stt(A1[:, :, s], modT[:, 8:16, s], 1.0, vT[:, 48:56], ALU.add, ALU.mult, ["modT", "vT"], ["A1"])
            stt(A2[:, :, s], modT[:, 32:40, s], 1.0, vT[:, 56:64], ALU.add, ALU.mult, ["modT", "vT"], ["A2"])
        P.end()
        P.hard = False

        def Bv(j, c, s):
            return modT[:, j * 8 + c, s:s + 1]

        def norm_mod(xg, W, Aq, jshift, s, hT, sq, rs, tmpb, ps_st, key, pskey, rst):
            for c in range(8):
                tt("pool", sq[:, c, 0:W], xg[:, c, 0:W], xg[:, c, 0:W], ALU.mult, [key], ["sq"])
            for c in range(8):
                mm(ps_st[:, 0:W], onesb[:], sq[:, c, 0:W], c == 0, c == 7, ["sq", "onesb"], [pskey])
            rstd_from_sum(rs[:, 0:W], ps_st[:, 0:W], float(D), [pskey], "rs", rst[:, 0:W])
            for c in range(8):
                tb = tmpb[c % 2]
                tt("dve", tb[:, 0:W], xg[:, c, 0:W], rs[:, 0:W], ALU.mult, [key, "rs"], [f"tmpb{c % 2}"])
                act(hT[:, c, 0:W], tb[:, 0:W], AF.Identity, [f"tmpb{c % 2}"], ["hT"], scale=Aq[:, c, s:s + 1], bias=Bv(jshift, c, s))

        wi = P.sbuf("wi", [128, 8, INW], BF16)
        for c in range(8):
            for hh in range(2):
                dma(wi[:, c, hh * 1408:(hh + 1) * 1408], w_in[l][c * 128:(c + 1) * 128, hh * 1408:(hh + 1) * 1408], [], ["wi"], "d_wi", eng="pool")
        wsr = P.sbuf("wsr", [128, 4, 128], BF16)
        dma(wsr[:], wsT_in[l].rearrange("g s t -> s g t"), [], ["wsr"], "d_ws", eng="pool")
        xg_p = [P.sbuf(f"xg{i}", [128, 8, 512], F32) for i in range(2)]
        cs_p = [P.sbuf(f"cs{i}", [128, 2, 512], F32) for i in range(2)]
        sq = P.sbuf("sq", [128, 8, 512], BF16)
        rs = P.sbuf("rs", [128, 512], F32)
        tmpb = [P.sbuf(f"tmpb{i}", [128, 512], F32) for i in range(2)]
        hT = P.sbuf("hT", [128, 8, 512], BF16)
        qf = [P.sbuf(f"qf{i}", [128, 512], F32) for i in range(2)]
        sqq_ = [P.sbuf(f"sqq{i}", [128, 512], BF16) for i in range(2)]
        lnq_ = [P.sbuf(f"lnq{i}", [128, 512], F32) for i in range(2)]
        rr_ = [P.sbuf(f"rr{i}", [128, 512], F32) for i in range(2)]
        qn_ = [P.sbuf(f"qn{i}", [128, 512], F32) for i in range(2)]
        qnb_ = [P.sbuf(f"qnb{i}", [128, 512], BF16) for i in range(2)]
        t1_ = [P.sbuf(f"t1{i}", [128, 512], F32) for i in range(2)]
        t2_ = [P.sbuf(f"t2{i}", [128, 512], F32) for i in range(2)]
        qo = [P.sbuf(f"qo{i}", [128, 512], BF16) for i in range(4)]
        z_ = [P.sbuf(f"z{i}", [128, 512], F32) for i in range(2)]
        zjunk_ = [P.sbuf(f"zjunk{i}", [128, 256], F32) for i in range(2)]
        ssv_ = [P.sbuf(f"ssv{i}", [128, 32], F32) for i in range(2)]
        ssw_ = [P.sbuf(f"ssw{i}", [128, 32], F32) for i in range(2)]
        ssv0_ = [P.sbuf(f"ssv0{i}", [128, 2], F32) for i in range(2)]
        for i in range(2):
            A("dve", lambda e, i=i: e.memset(ssv_[i][:], 1.0), [], [f"ssv{i}"])
        vnb_ = [P.sbuf(f"vnb{i}", [128, 256], BF16) for i in range(2)]
        ob_ = [P.sbuf(f"ob{i}", [128, 256], BF16) for i in range(2)]
        obT = P.sbuf("obT", [128, 2, 512], BF16)
        vt = [P.sbuf(f"vt{i}", [128, 768], BF16) for i in range(2)]
        rst = P.sbuf("rst", [128, 512], F32)
        ps_q = [P.psum(f"ps_q{i}", [128, 512]) for i in range(2)]
        ps_n = P.psum("ps_n", [128, 512])
        ps_r = P.psum("ps_r", [128, 512])
        ps_uv = P.psum("ps_uv", [128, 512])
        ps_va = P.psum("ps_va", [128, 512])
        ps_vc = P.psum("ps_vc", [128, 512])
        ps_tb = P.psum("ps_tb", [128, 256], BF16)

        ftiles = [(0, "a", 64, "q", 0), (128, "a", 64, "q", 1)] + [(256 + 128 * h, "c", 66, "q", 2 + h) for h in range(4)] + \
                 [(1280, "a", 65, "k", 0), (1408, "a", 65, "k", 1)] + [(1792 + 128 * h, "c", 67, "k", 2 + h) for h in range(4)]

        def load_group(gi):
            g, W, off, isctx = groups[gi]
            b = gi % 2
            dma(xg_p[b][:, :, 0:W], xT[:, off:off + W].rearrange("(c p) n -> p c n", p=128), [("xT", g)], [f"xg{b}"], f"d_xg{b}")
            if not isctx:
                dma(cs_p[b][:, 0, :], cos_in[:, off:off + W], [], [f"cs{b}"], f"d_cs{b}")
                dma(cs_p[b][:, 1, :], sin_in[:, off:off + W], [], [f"cs{b}"], f"d_cs{b}")

        def gather(g):
            A("pool", lambda e: e.collective_compute("AllGather", ALU.bypass, replica_groups=PAIRS, ins=[kT_in_t[g].ap().opt()], outs=[kT_all_t[g].ap().opt()]),
              [("kT_in", g)], [("kT_all", g)], dsem="d_cc", dinc=1)
            A("pool", lambda e: e.collective_compute("AllGather", ALU.bypass, replica_groups=PAIRS, ins=[V_in_t[g].ap().opt()], outs=[V_all_t[g].ap().opt()]),
              [("V_in", g)], [("V_all", g)], dsem="d_cc", dinc=1)

        load_group(0)
        nq = 0
        nv = 0
        for gi, (g, W, off, isctx) in enumerate(groups):
            if gi + 1 < len(groups):
                load_group(gi + 1)
            b = gi % 2
            xg = xg_p[b]
            s = 1 if isctx else 0
            norm_mod(xg, W, A1, 0, s, hT, sq, rs, tmpb, ps_r, f"xg{b}", "ps_r", rst)
            atiles = [ft for ft in ftiles if not (isctx and last and ft[3] == "q")]

            def proj(i):
                co = atiles[i][0]
                k = (nq + i) % 2
                for c in range(8):
                    mm(ps_q[k][:, 0:W], wi[:, c, co:co + 128], hT[:, c, 0:W], c == 0, c == 7, ["wi", "hT"], [f"ps_q{k}"])

            proj(0)
            nq0 = nq
            for ti, (co, kind, gcol, dst, dti) in enumerate(atiles):
                nqi = nq0 + ti
                k2 = nqi % 2
                pq = ps_q[k2]
                pqk = f"ps_q{k2}"
                qfb = qf[k2]
                qfk = f"qf{k2}"
                act(sqq_[k2][:, 0:W], pq[:, 0:W], AF.Square, [pqk], [f"sqq{k2}"])
                if ti + 1 < len(atiles):
                    proj(ti + 1)
                mm(ps_n[:, 0:W], blk[:], sqq_[k2][:, 0:W], True, True, [f"sqq{k2}", "blk"], ["ps_n"])
                act(lnq_[k2][:, 0:W], ps_n[:, 0:W], AF.Ln, ["ps_n"], [f"lnq{k2}"], scale=1.0 / 64, bias=EPS)
                act(rr_[k2][:, 0:W], lnq_[k2][:, 0:W], AF.Exp, [f"lnq{k2}"], [f"rr{k2}"], scale=-0.5)
                qob = qo[nqi % 4]
                qok = f"qo{nqi % 4}"
                rope = (kind == "c") and not isctx
                if not rope:
                    stt(qob[:, 0:W], pq[:, 0:W], vT[:, gcol:gcol + 1], rr_[k2][:, 0:W], ALU.mult, ALU.mult, [pqk, f"rr{k2}", "vT"], [qok])
                else:
                    stt(qn_[k2][:, 0:W], pq[:, 0:W], vT[:, gcol:gcol + 1], rr_[k2][:, 0:W], ALU.mult, ALU.mult, [pqk, f"rr{k2}", "vT"], [f"qn{k2}"])
                    cp("act", qnb_[k2][:, 0:W], qn_[k2][:, 0:W], [f"qn{k2}"], [f"qnb{k2}"])
                    mm(ps_r[:, 0:W], rperm[:], qnb_[k2][:, 0:W], True, True, [f"qnb{k2}", "rperm"], ["ps_r"])
                    tt("pool", t1_[k2][:, 0:W], qn_[k2][:, 0:W], cs_p[b][:, 0, 0:W], ALU.mult, [f"qn{k2}", f"cs{b}"], [f"t1{k2}"])
                    tt("dve", t2_[k2][:, 0:W], ps_r[:, 0:W], cs_p[b][:, 1, 0:W], ALU.mult, ["ps_r", f"cs{b}"], [f"t2{k2}"])
                    tt("dve", qob[:, 0:W], t1_[k2][:, 0:W], t2_[k2][:, 0:W], ALU.add, [f"t1{k2}", f"t2{k2}"], [qok])
                if dst == "q":
                    dma(qT[dti * 128:(dti + 1) * 128, off:off + W], qob[:, 0:W], [qok], [("qT", g)], f"d_qo{nqi % 4}")
                elif isctx:
                    dma(kTc[dti * 128:(dti + 1) * 128, :], qob[:, 0:W], [qok], ["kTc"], f"d_qo{nqi % 4}")
                else:
                    dma(kT_in[g][dti * 128:(dti + 1) * 128, :], qob[:, 0:W], [qok], [("kT_in", g)], f"d_qo{nqi % 4}")
            nq = nq0 + len(atiles)
            gate_on = not (isctx and last)
            nv0 = nv

            def stage_a(t4):
                tsl = slice(t4 * 128, (t4 + 1) * 128)
                kv = (nv0 + t4) % 2
                vtb, vtk = vt[kv], f"vt{kv}"
                for c in range(8):
                    mm(ps_va[:, 0:256], hT[:, c, tsl], wi[:, c, 1536:1792], c == 0, c == 7, ["wi", "hT"], ["ps_va"])
                for c in range(8):
                    mm(ps_vc[:, :], hT[:, c, tsl], wi[:, c, 2304:2816], c == 0, c == 7, ["wi", "hT"], ["ps_vc"])
                cp("act", vtb[:, 0:256], ps_va[:, 0:256], ["ps_va"], [vtk])
                cp("act", vtb[:, 256:768], ps_vc[:, :], ["ps_vc"], [vtk])
                if isctx:
                    dma(Vc[tsl, :], vtb[:], [vtk], ["Vc"], f"d_vt{kv}")
                else:
                    dma(V_in[g][t4 * 128:(t4 + 1) * 128, :], vtb[:], [vtk], [("V_in", g)], f"d_vt{kv}")
                if not gate_on:
                    return
                for c in range(8):
                    mm(ps_uv[:, :], hT[:, c, tsl], wi[:, c, 768:1280], c == 0, c == 7, ["wi", "hT"], ["ps_uv"])
                act(z_[kv][:], ps_uv[:], AF.Gelu_apprx_tanh, ["ps_uv"], [f"z{kv}"])
                tt("pool", zjunk_[kv][:], z_[kv][:, 256:512], z_[kv][:, 256:512], ALU.mult, [f"z{kv}"], [f"zjunk{kv}"])
                A("dve", lambda e, k=kv: e.tensor_reduce(out=ssv0_[k][:, 0:1], in_=zjunk_[k][:], axis=mybir.AxisListType.X, op=ALU.add), [f"zjunk{kv}"], [f"ssv0{kv}"])
                P.hard = True
                cp("dve", ssv_[kv][:, 0:1], ssv0_[kv][:, 0:1], [f"ssv0{kv}"], [f"ssv{kv}"])
                act(ssw_[kv][:, 0:32], ssv_[kv][:, 0:32], AF.Ln, [f"ssv{kv}"], [f"ssw{kv}"], scale=1.0 / 256, bias=EPS)
                act(ssw_[kv][:, 0:32], ssw_[kv][:, 0:32], AF.Exp, [f"ssw{kv}"], [f"ssw{kv}"], scale=-0.5)
                P.hard = False
                stt(vnb_[kv][:], z_[kv][:, 256:512], ssw_[kv][:, 0:1], vgB[:], ALU.mult, ALU.mult, [f"z{kv}", f"ssw{kv}", "vgB"], [f"vnb{kv}"])

            def stage_b(t4):
                tsl = slice(t4 * 128, (t4 + 1) * 128)
                kv = (nv0 + t4) % 2
                for gg in range(4):
                    mm(ps_va[:, 256 + gg * 64:256 + (gg + 1) * 64], wsr[:, gg, :], vnb_[kv][:, gg * 64:(gg + 1) * 64], True, True, ["wsr", f"vnb{kv}"], ["ps_va"])
                for gg in range(4):
                    stt(ob_[kv][:, gg * 64:(gg + 1) * 64], ps_va[:, 256 + gg * 64:256 + (gg + 1) * 64], vT[:, 69 + gg:70 + gg], z_[kv][:, gg * 64:(gg + 1) * 64],
                        ALU.add, ALU.mult, ["ps_va", f"z{kv}", "vT"], [f"ob{kv}"])
                for j in range(2):
                    tr(ps_tb[:, j * 128:(j + 1) * 128], ob_[kv][:, j * 128:(j + 1) * 128], identb[:], [f"ob{kv}", "identb"], ["ps_tb"])
                cp("act", obT[:, :, tsl], ps_tb[:].rearrange("p (j n) -> p j n", j=2), ["ps_tb"], ["obT"])

            ntt = W // 128
            stage_a(0)
            for t4 in range(ntt):
                if t4 + 1 < ntt:
                    stage_a(t4 + 1)
                if gate_on:
                    stage_b(t4)
            nv = nv0 + ntt
            if not (isctx and last):
                dma(mixT[256:512, off:off + W].rearrange("(j p) n -> p j n", p=128), obT[:, :, 0:W], ["obT"], [("mixT", g)], "d_obT")
            if gi >= 1:
                gather(gi - 1)
        if debug and l == 0:
            dma(dbg_k, kT_all[0], [("kT_all", 0)], [], "d_dbg")
            dma(dbg_v, V_all[0], [("V_all", 0)], [], "d_dbg")
        P.end()
        if stop_after == f"P{l}":
            P.close()
            return nc

        kb = P.sbuf("kb", [128, (NB + 2) * 128], BF16)
        vb = P.sbuf("vb", [128, NB + 2, 128], BF16)
        tbl = [P.sbuf(f"tbl{i}", [128, NSET * 8, 512], F32) for i in range(2)]
        qa = [P.sbuf(f"qa{i}", [128, 512], BF16) for i in range(2)]
        sb = [P.sbuf(f"sbias{i}", [128, 512], F32) for i in range(4)]
        pt = [P.sbuf(f"pt{i}", [128, 512], BF16) for i in range(4)]
        rcp = P.sbuf("rcp", [128, 512], F32)
        oa = [P.sbuf(f"oa{i}", [128, 512], BF16) for i in range(2)]
        ps_s = [P.psum(f"ps_s{i}", [128, 512]) for i in range(4)]
        ps_o = [P.psum(f"ps_o{i}", [128, 512]) for i in range(2)]
        ps_d = [P.psum(f"ps_d{i}", [128, 512]) for i in range(2)]
        nql = 0
        nt = 0
        for t in range(2):
            dma(kb[:, 0:256], kT_all[NG - 1][t * 128:(t + 1) * 128, 256:512], [], ["kb"], "d_kb")
            for g2 in range(NG):
                dma(kb[:, 256 + g2 * 512:256 + (g2 + 1) * 512], kT_in[g2][t * 128:(t + 1) * 128, :], [], ["kb"], "d_kb")
            dma(kb[:, 256 + NT:512 + NT], kT_all[0][768 + t * 128:768 + (t + 1) * 128, 0:256], [], ["kb"], "d_kb")
            dma(kb[:, NB * 128:(NB + 2) * 128], kTc[t * 128:(t + 1) * 128, :], [], ["kb"], "d_kb")
            dma(vb[:, 0:2, :], V_all[NG - 1][256:512, t * 128:(t + 1) * 128].rearrange("(k p) c -> p k c", p=128), [], ["vb"], "d_vb")
            for g2 in range(NG):
                dma(vb[:, 2 + g2 * 4:2 + (g2 + 1) * 4, :], V_in[g2][:, t * 128:(t + 1) * 128].rearrange("(k p) c -> p k c", p=128), [], ["vb"], "d_vb")
            dma(vb[:, 2 + NTT:NB, :], V_all[0][512:768, t * 128:(t + 1) * 128].rearrange("(k p) c -> p k c", p=128), [], ["vb"], "d_vb")
            dma(vb[:, NB:NB + 2, :], Vc[:, t * 128:(t + 1) * 128].rearrange("(k p) c -> p k c", p=128), [], ["vb"], "d_vb")
            for hh in range(2):
                dma(tbl[hh][:], nab[l, 2 * t + hh].rearrange("v p n -> p v n"), [], [f"tbl{hh}"], f"d_tbl{hh}")
            for (g, W, off, isctx) in act_groups:
                qab = qa[nql % 2]
                qak = f"qa{nql % 2}"
                dma(qab[:, 0:W], qT[t * 128:(t + 1) * 128, off:off + W], [("qT", g)], [qak], f"d_qa{nql % 2}")
                pso, psd = ps_o[nql % 2], ps_d[nql % 2]
                pok, pdk = f"ps_o{nql % 2}", f"ps_d{nql % 2}"
                for hh in range(2):
                    pl = slice(hh * 64, (hh + 1) * 64)
                    if isctx:
                        tiles = [(NB, None), (NB + 1, None)]
                    else:
                        vset = 0 if NSET == 1 else (0 if g == 0 else (2 if g == NG - 1 else 1))
                        tiles = [(g * 4 + i, vset * 8 + i) for i in range(8)] + [(NB, None), (NB + 1, None)]
                    base = nt

                    def smm_na(idx):
                        bt, var = tiles[idx]
                        k = (base + idx) % 4
                        mm(ps_s[k][:, 0:W], kb[pl, bt * 128:(bt + 1) * 128], qab[pl, 0:W], True, True, ["kb", qak], [f"ps_s{k}"])

                    for j0 in range(min(2, len(tiles))):
                        smm_na(j0)
                    for idx, (bt, var) in enumerate(tiles):
                        if idx + 2 < len(tiles):
                            smm_na(idx + 2)
                        k = (base + idx) % 4
                        pss, psk, ptb, ptk = ps_s[k], f"ps_s{k}", pt[k], f"pt{k}"
                        if var is None:
                            act(ptb[:, 0:W], pss[:, 0:W], AF.Exp, [psk], [ptk], scale=0.125)
                        else:
                            sbb = sb[k]
                            stt(sbb[:, 0:W], pss[:, 0:W], 0.125, tbl[hh][:, var, 0:W], ALU.mult, ALU.add, [psk, f"tbl{hh}"], [f"sbias{k}"])
                            act(ptb[:, 0:W], sbb[:, 0:W], AF.Exp, [f"sbias{k}"], [ptk])
                        first, lastt = idx == 0, idx == len(tiles) - 1
                        mm(pso[pl, 0:W], vb[:, bt, hh * 64:(hh + 1) * 64], ptb[:, 0:W], first, lastt, ["vb", ptk], [pok])
                        mm(psd[pl, 0:W], onesb[:, 0:64], ptb[:, 0:W], first, lastt, ["onesb", ptk], [pdk])
                        nt += 1
                A("dve", lambda e, psd=psd, W=W: e.reciprocal(out=rcp[:, 0:W], in_=psd[:, 0:W]), [pdk], ["rcp"])
                oab = oa[nql % 2]
                tt("dve", oab[:, 0:W], pso[:, 0:W], rcp[:, 0:W], ALU.mult, [pok, "rcp"], [f"oa{nql % 2}"])
                dma(mixT[t * 128:(t + 1) * 128, off:off + W], oab[:, 0:W], [f"oa{nql % 2}"], [("mixT", g)], f"d_oa{nql % 2}")
                nql += 1
        P.end()
        if stop_after == f"NA{l}":
            P.close()
            return nc

        kd = [P.sbuf(f"kd{i}", [128, (KT + 2) * 128], BF16) for i in range(2)]
        vd = [P.sbuf(f"vd{i}", [128, KT + 2, 128], BF16) for i in range(2)]
        qd = [P.sbuf(f"qd{i}", [128, 512], BF16) for i in range(2)]
        ptd = [P.sbuf(f"ptd{i}", [128, 2, 512], BF16) for i in range(3)]
        r12 = P.sbuf("r12", [128, 2, 512], F32)
        o1 = P.sbuf("o1", [128, 512], F32)
        o2 = P.sbuf("o2", [128, 512], F32)
        osq = P.sbuf("osq", [128, 512], BF16)
        lno = P.sbuf("lno", [128, 512], F32)
        oc = [P.sbuf(f"oc{i}", [128, 512], BF16) for i in range(2)]
        ps_sd = [P.psum(f"ps_sd{i}", [128, 2, 512]) for i in range(2)]
        ps_od = [P.psum(f"ps_od{m}", [128, 512]) for m in range(2)]
        ps_dd = P.psum("ps_dd", [128, 2, 512])
        dacc = P.sbuf("dacc", [128, 2, 512], F32)

        def load_head(h):
            b = h % 2
            for r in range(2):
                for g2 in range(NG):
                    dma(kd[b][:, (r * NG + g2) * 512:(r * NG + g2 + 1) * 512], kT_all[g2][r * 768 + (2 + h) * 128: r * 768 + (3 + h) * 128, :], [], [f"kd{b}"], f"d_kd{b}")
            dma(kd[b][:, KT * 128:(KT + 2) * 128], kTc[(2 + h) * 128:(3 + h) * 128, :], [], [f"kd{b}"], f"d_kd{b}")
            for r in range(2):
                for g2 in range(NG):
                    dma(vd[b][:, (r * NG + g2) * 4:(r * NG + g2 + 1) * 4, :], V_all[g2][r * 512:(r + 1) * 512, 256 + h * 128:256 + (h + 1) * 128].rearrange("(k p) c -> p k c", p=128),
                        [], [f"vd{b}"], f"d_vd{b}")
            dma(vd[b][:, KT:KT + 2, :], Vc[:, 256 + h * 128:256 + (h + 1) * 128].rearrange("(k p) c -> p k c", p=128), [], [f"vd{b}"], f"d_vd{b}")

        load_head(0)
        nql = 0
        for h in range(4):
            if h + 1 < 4:
                load_head(h + 1)
            b = h % 2
            kdb, vdb = kd[b], vd[b]
            for (g, W, off, isctx) in act_groups:
                qdb = qd[nql % 2]
                qdk = f"qd{nql % 2}"
                dma(qdb[:, 0:W], qT[(2 + h) * 128:(3 + h) * 128, off:off + W], [("qT", g)], [qdk], f"d_qd{nql % 2}")
                ktiles = [KT, KT + 1] if isctx else list(range(KT + 2))

                def smm(idx):
                    kt = ktiles[idx]
                    for m in range(2):
                        pl = slice(m * 64, (m + 1) * 64)
                        mm(ps_sd[idx % 2][:, m, 0:W], kdb[pl, kt * 128:(kt + 1) * 128], qdb[pl, 0:W], True, True, [f"kd{b}", qdk], [f"ps_sd{idx % 2}"])

                smm(0)
                for idx, kt in enumerate(ktiles):
                    if idx + 1 < len(ktiles):
                        smm(idx + 1)
                    first, lastt = idx == 0, idx == len(ktiles) - 1
                    ptb = ptd[idx % 3]
                    ptk = f"ptd{idx % 3}"
                    act(ptb[:, :, 0:W], ps_sd[idx % 2][:, :, 0:W], AF.Exp, [f"ps_sd{idx % 2}"], [ptk], scale=0.125)
                    for m in range(2):
                        mm(ps_od[m][:, 0:W], vdb[:, kt, :], ptb[:, m, 0:W], first, lastt, [f"vd{b}", ptk], [f"ps_od{m}"])
                    if first:
                        cp("dve", ps_dd[:, :, 0:W], ptb[:, :, 0:W], [ptk], ["ps_dd"])
                    else:
                        tt("dve", ps_dd[:, :, 0:W], ps_dd[:, :, 0:W], ptb[:, :, 0:W], ALU.add, [ptk, "ps_dd"], ["ps_dd"])
                cp("act", dacc[:, :, 0:W], ps_dd[:, :, 0:W], ["ps_dd"], ["dacc"])
                for m in range(2):
                    mm(ps_dd[:, m, 0:W], onesf[:], dacc[:, m, 0:W], True, True, ["onesf", "dacc"], ["ps_dd"])
                act(r12[:, :, 0:W], ps_dd[:, :, 0:W], AF.Ln, ["ps_dd"], ["r12"])
                act(r12[:, :, 0:W], r12[:, :, 0:W], AF.Exp, ["r12"], ["r12"], scale=-1.0)
                tt("dve", o1[:, 0:W], ps_od[0][:, 0:W], r12[:, 0, 0:W], ALU.mult, ["ps_od0", "r12"], ["o1"])
                tt("dve", o2[:, 0:W], ps_od[1][:, 0:W], r12[:, 1, 0:W], ALU.mult, ["ps_od1", "r12"], ["o2"])
                stt(o1[:, 0:W], o2[:, 0:W], nlam[:, 0:1], o1[:, 0:W], ALU.mult, ALU.add, ["o1", "o2", "nlam"], ["o1"])
                tt("pool", osq[:, 0:W], o1[:, 0:W], o1[:, 0:W], ALU.mult, ["o1"], ["osq"])
                pn = ps_sd[0]
                mm(pn[:, 0, 0:W], onesb[:], osq[:, 0:W], True, True, ["osq", "onesb"], ["ps_sd0"])
                act(lno[:, 0:W], pn[:, 0, 0:W], AF.Ln, ["ps_sd0"], ["lno"], scale=1.0 / 128, bias=EPS)
                act(lno[:, 0:W], lno[:, 0:W], AF.Exp, ["lno"], ["lno"], scale=-0.5)
                ocb = oc[nql % 2]
                stt(ocb[:, 0:W], o1[:, 0:W], sgl[:, 0:1], lno[:, 0:W], ALU.mult, ALU.mult, ["o1", "lno", "sgl"], [f"oc{nql % 2}"])
                dma(mixT[512 + h * 128:512 + (h + 1) * 128, off:off + W], ocb[:, 0:W], [f"oc{nql % 2}"], [("mixT", g)], f"d_oc{nql % 2}")
                nql += 1
        P.end()
        if stop_after == f"DA{l}":
            P.close()
            return nc

        wo = P.sbuf("wo", [128, 8, D], BF16)
        for c in range(8):
            dma(wo[:, c, :], w_out[l][c * 128:(c + 1) * 128, :], [], ["wo"], "d_wo", eng="pool")
        mx = [P.sbuf(f"mx{i}", [128, 8, 512], BF16) for i in range(2)]
        xg_o = [P.sbuf(f"xgo{i}", [128, 8, 512], F32) for i in range(2)]
        sq = P.sbuf("sq", [128, 8, 512], BF16)
        rs = P.sbuf("rs", [128, 512], F32)
        tmpb = [P.sbuf(f"tmpb{i}", [128, 512], F32) for i in range(2)]
        h2 = [P.sbuf(f"h2{i}", [128, 8, 512], BF16) for i in range(2)]
        ps_y = [P.psum(f"ps_y{i}", [128, 512]) for i in range(2)]
        ps_st = P.psum("ps_st", [128, 512])
        rst = P.sbuf("rst", [128, 512], F32)

        def load_o1(gi):
            g, W, off, isctx = act_groups[gi]
            b = gi % 2
            dma(mx[b][:, :, 0:W], mixT[:, off:off + W].rearrange("(c p) n -> p c n", p=128), [("mixT", g)], [f"mx{b}"], f"d_mx{b}")
            dma(xg_o[b][:, :, 0:W], xT[:, off:off + W].rearrange("(c p) n -> p c n", p=128), [("xT", g)], [f"xgo{b}"], f"d_xgo{b}")

        load_o1(0)
        for gi, (g, W, off, isctx) in enumerate(act_groups):
            if gi + 1 < len(act_groups):
                load_o1(gi + 1)
            b = gi % 2
            s = 1 if isctx else 0
            xg = xg_o[b]
            for c in range(8):
                py = ps_y[c % 2]
                for kc in range(8):
                    mm(py[:, 0:W], wo[:, kc, c * 128:(c + 1) * 128], mx[b][:, kc, 0:W], kc == 0, kc == 7, ["wo", f"mx{b}"], [f"ps_y{c % 2}"])
                stt(xg[:, c, 0:W], py[:, 0:W], Bv(2, c, s), xg[:, c, 0:W], ALU.mult, ALU.add, [f"ps_y{c % 2}", f"xgo{b}", "modT"], [f"xgo{b}"])
            dma(xT[:, off:off + W].rearrange("(c p) n -> p c n", p=128), xg[:, :, 0:W], [f"xgo{b}"], [("xT", g)], f"d_xs{b}")
            hb = h2[b]
            norm_mod(xg, W, A2, 3, s, hb, sq, rs, tmpb, ps_st, f"xgo{b}", "ps_st", rst)
            dma(h2T[:, off:off + W].rearrange("(c p) n -> p c n", p=128), hb[:, :, 0:W], ["hT"], [("h2T", g)], f"d_h2{b}")
        P.end()
        if stop_after == f"O1{l}":
            P.close()
            return nc

        w1s = P.sbuf("w1s", [128, 8, DFF], BF16)
        w3s = P.sbuf("w3s", [128, 8, DFF], BF16)
        for c in range(8):
            for hh in range(2):
                dma(w1s[:, c, hh * 1408:(hh + 1) * 1408], w1[l][c * 128:(c + 1) * 128, hh * 1408:(hh + 1) * 1408], [], ["w1s"], "d_w1", eng="pool")
                dma(w3s[:, c, hh * 1408:(hh + 1) * 1408], w3[l][c * 128:(c + 1) * 128, hh * 1408:(hh + 1) * 1408], [], ["w3s"], "d_w3", eng="pool")
        h2i = [P.sbuf(f"h2i{i}", [128, 8, 512], BF16) for i in range(2)]
        sl = [P.sbuf(f"sl{i}", [128, 512], F32) for i in range(2)]
        gb = [P.sbuf(f"gb{i}", [128, 2, 512], BF16) for i in range(2)]
        ps_a = [P.psum(f"ps_a{i}", [128, 512]) for i in range(2)]
        ps_b = [P.psum(f"ps_b{i}", [128, 512]) for i in range(2)]

        def load_o2(gi):
            g, W, off, isctx = act_groups[gi]
            dma(h2i[gi % 2][:, :, 0:W], h2T[:, off:off + W].rearrange("(c p) n -> p c n", p=128), [("h2T", g)], [f"h2i{gi % 2}"], f"d_h2i{gi % 2}")

        load_o2(0)
        nj = 0
        for gi, (g, W, off, isctx) in enumerate(act_groups):
            if gi + 1 < len(act_groups):
                load_o2(gi + 1)
            hb = h2i[gi % 2]
            hk = f"h2i{gi % 2}"
            for j in range(DFF // 128):
                pa, pb = ps_a[j % 2], ps_b[j % 2]
                for kc in range(8):
                    mm(pa[:, 0:W], w1s[:, kc, j * 128:(j + 1) * 128], hb[:, kc, 0:W], kc == 0, kc == 7, ["w1s", hk], [f"ps_a{j % 2}"])
                for kc in range(8):
                    mm(pb[:, 0:W], w3s[:, kc, j * 128:(j + 1) * 128], hb[:, kc, 0:W], kc == 0, kc == 7, ["w3s", hk], [f"ps_b{j % 2}"])
                slb = sl[j % 2]
                act(slb[:, 0:W], pa[:, 0:W], AF.Silu, [f"ps_a{j % 2}"], [f"sl{j % 2}"])
                gbb = gb[(nj // 2) % 2]
                tt("dve", gbb[:, j % 2, 0:W], slb[:, 0:W], pb[:, 0:W], ALU.mult, [f"sl{j % 2}", f"ps_b{j % 2}"], [f"gb{(nj // 2) % 2}"])
                if j % 2 == 1:
                    dma(gT[(j - 1) * 128:(j + 1) * 128, off:off + W].rearrange("(c p) n -> p c n", p=128), gbb[:, :, 0:W], [f"gb{(nj // 2) % 2}"], [("gT", g)],
                        f"d_gb{(nj // 2) % 2}")
                nj += 1
        P.end()
        if stop_after == f"O2{l}":
            P.close()
            return nc

        w2s = P.sbuf("w2s", [128, DFF // 128, D], BF16)
        for j in range(DFF // 128):
            dma(w2s[:, j, :], w2[l][j * 128:(j + 1) * 128, :], [], ["w2s"], "d_w2", eng="pool")
        gi_b = [P.sbuf(f"gi{i}", [128, DFF // 128, 512], BF16) for i in range(2)]
        xg_f = [P.sbuf(f"xgf{i}", [128, 8, 512], F32) for i in range(2)]
        ot = [P.sbuf(f"ot{i}", [128, D], F32) for i in range(2)]
        ps_y = [P.psum(f"ps_y{i}", [128, 512]) for i in range(2)]
        ps_t = [P.psum(f"ps_t{i}", [128, 512]) for i in range(4)]

        def load_o3(gi):
            g, W, off, isctx = act_groups[gi]
            b = gi % 2
            for hh in range(2):
                dma(gi_b[b][:, hh * 11:(hh + 1) * 11, 0:W], gT[hh * 1408:(hh + 1) * 1408, off:off + W].rearrange("(c p) n -> p c n", p=128), [("gT", g)], [f"gi{b}"], f"d_gi{b}")
            dma(xg_f[b][:, :, 0:W], xT[:, off:off + W].rearrange("(c p) n -> p c n", p=128), [("xT", g)], [f"xgf{b}"], f"d_xgf{b}")

        load_o3(0)
        n = 0
        for gi, (g, W, off, isctx) in enumerate(act_groups):
            if gi + 1 < len(act_groups):
                load_o3(gi + 1)
            b = gi % 2
            s = 1 if isctx else 0
            xg = xg_f[b]
            for c in range(8):
                py = ps_y[c % 2]
                for j in range(DFF // 128):
                    mm(py[:, 0:W], w2s[:, j, c * 128:(c + 1) * 128], gi_b[b][:, j, 0:W], j == 0, j == DFF // 128 - 1, ["w2s", f"gi{b}"], [f"ps_y{c % 2}"])
                stt(xg[:, c, 0:W], py[:, 0:W], Bv(5, c, s), xg[:, c, 0:W], ALU.mult, ALU.add, [f"ps_y{c % 2}", f"xgf{b}", "modT"], [f"xgf{b}"])
            if not last:
                dma(xT[:, off:off + W].rearrange("(c p) n -> p c n", p=128), xg[:, :, 0:W], [f"xgf{b}"], [("xT", g)], f"d_xs{b}")
            else:
                for t4 in range(W // 128):
                    otb = ot[n % 2]
                    for bb in range(2):
                        pb = ps_t[(n % 2) * 2 + bb]
                        for c4 in range(4):
                            c = bb * 4 + c4
                            tr(pb[:, c4 * 128:(c4 + 1) * 128], xg[:, c, t4 * 128:(t4 + 1) * 128], identf[:], [f"xgf{b}", "identf"], [f"ps_t{(n % 2) * 2 + bb}"])
                        cp("act" if bb == 0 else "dve", otb[:, bb * 512:(bb + 1) * 512], pb[:], [f"ps_t{(n % 2) * 2 + bb}"], [f"ot{n % 2}"])
                    dma(out[off + t4 * 128: off + (t4 + 1) * 128, :], otb[:], [f"ot{n % 2}"], [], f"d_ot{n % 2}")
                    n += 1
        P.end()
    P.close()
    return nc


def _rope_tables(S, half, NT):
    t = np.arange(half * NT, (half + 1) * NT, dtype=np.int32)
    row = (t // 64).astype(np.float32)
    col = (t % 64).astype(np.float32)
    inv = (np.float32(10000.0) ** (-np.arange(0, 32, 2, dtype=np.float32) / np.float32(32))).astype(np.float32)
    ar = row[:, None] * inv[None, :]
    ac = col[:, None] * inv[None, :]
    ang = np.concatenate([ar, ar, ac, ac], axis=-1)
    cos = np.cos(ang).astype(np.float32)
    sin = np.sin(ang).astype(np.float32)
    sign = np.ones(64, np.float32)
    for a in range(2):
        sign[a * 32:a * 32 + 16] = -1.0
    sin = sin * sign[None, :]
    cosT = np.ascontiguousarray(np.concatenate([cos.T, cos.T], axis=0))
    sinT = np.ascontiguousarray(np.concatenate([sin.T, sin.T], axis=0))
    return cosT, sinT


def _rperm():
    m = np.zeros((128, 128), np.float32)
    for base in (0, 64):
        for dp in range(64):
            tq = (dp % 32) // 16
            src = dp + 16 if tq == 0 else dp - 16
            m[base + src, base + dp] = 1.0
    return m


def _na_bias_tables(rpb, S, half, NT, NG):
    rows = S // 64
    NSET = 1 if NG == 1 else 3
    out = np.full((4, NSET * 8, 128, 512), -1e30, np.float32)
    kh = 8
    r_all = np.arange(rows)
    kr0 = np.clip(r_all - kh // 2, 0, rows - kh)
    cq = np.arange(64)
    c0 = np.clip(cq - 8, 0, 64 - 16)
    sets = [0] if NG == 1 else [0, 1, 2]
    for si, sset in enumerate(sets):
        if NG == 1:
            g = 0
        else:
            g = 0 if sset == 0 else (NG - 1 if sset == 2 else 1)
        R = half * (rows // 2) + g * 8
        qrow = R + np.arange(512) // 64
        qcol = np.arange(512) % 64
        for i in range(8):
            bt = g * 4 + i
            kt = half * (NT // 128) + bt - 2
            if kt < 0 or kt >= S // 128:
                continue
            krow = kt * 2 + np.arange(128) // 64
            kcol = np.arange(128) % 64
            rv = (krow[:, None] >= kr0[qrow][None, :]) & (krow[:, None] < kr0[qrow][None, :] + kh)
            cvd = (kcol[:, None] >= c0[qcol][None, :]) & (kcol[:, None] < c0[qcol][None, :] + 16)
            roff = np.clip(krow[:, None] - qrow[None, :] + 7, 0, 14)
            coff = np.clip(kcol[:, None] - qcol[None, :], -15, 15) + 15
            valid = rv & cvd
            for h in range(4):
                b = rpb[h][roff, coff]
                out[h, si * 8 + i] = np.where(valid, b, np.float32(-1e30))
    return out


def prepare_inputs(inp, S):
    NT = S // 2
    NG = NT // 512
    L = inp["w_mod"].shape[0]
    f = lambda a: np.ascontiguousarray(np.asarray(a, dtype=np.float32))
    x, c, ctx, c_ctx = f(inp["x"]), f(inp["c"]), f(inp["ctx"]), f(inp["c_ctx"])
    vecs = np.zeros((L, 128, 128), np.float32)
    for l in range(L):
        vecs[l, 0:48] = f(inp["b_mod"])[l].reshape(48, 128)
        vecs[l, 48:56] = f(inp["norm1_g"])[l].reshape(8, 128)
        vecs[l, 56:64] = f(inp["norm2_g"])[l].reshape(8, 128)
        vecs[l, 64] = np.tile(f(inp["na_q_g"])[l], 2)
        vecs[l, 65] = np.tile(f(inp["na_k_g"])[l], 2)
        vecs[l, 66] = np.tile(f(inp["da_q_g"])[l], 2)
        vecs[l, 67] = np.tile(f(inp["da_k_g"])[l], 2)
        vecs[l, 68] = f(inp["da_sub_g"])[l]
        vecs[l, 69:73] = f(inp["gm_bs"])[l]
    lamv = np.stack([np.concatenate([f(inp[k])[l] for k in ("da_lq1", "da_lk1", "da_lq2", "da_lk2")]) for l in range(L)])[:, None, :]
    vg = f(inp["gm_v_g"])[:, None, :]
    wsT = np.ascontiguousarray(np.transpose(f(inp["gm_ws"]), (0, 1, 3, 2)))
    common = dict(vecs=vecs, lamv=np.ascontiguousarray(lamv), vg=np.ascontiguousarray(vg), w_mod=f(inp["w_mod"]), w_in=f(inp["w_in"]),
                  w_out=f(inp["w_out"]), w1=f(inp["ffn_w1"]), w3=f(inp["ffn_w3"]), w2=f(inp["ffn_w2"]), wsT=wsT,
                  ident=np.eye(128, dtype=np.float32), rperm=_rperm(),
                  blk=np.kron(np.eye(2, dtype=np.float32), np.ones((64, 64), np.float32)))
    rpb = f(inp["na_rpb"])
    maps = []
    tabs = {}
    for core in range(8):
        b, half = core // 2, core % 2
        if half not in tabs:
            cosT, sinT = _rope_tables(S, half, NT)
            nabt = np.stack([_na_bias_tables(rpb[l], S, half, NT, NG) for l in range(L)])
            tabs[half] = (cosT, sinT, nabt)
        cosT, sinT, nabt = tabs[half]
        m = dict(common)
        m["x"] = np.ascontiguousarray(x[b, half * NT:(half + 1) * NT])
        m["ctx"] = np.ascontiguousarray(ctx[b])
        m["cvec"] = np.ascontiguousarray(np.concatenate([c[b].reshape(8, 128), c_ctx.reshape(8, 128)], axis=0))
        m["cos"], m["sin"], m["nab"] = cosT, sinT, nabt
        maps.append(m)
    return maps


_CACHE = {}


def kernel(**inputs):
    S = int(np.asarray(inputs["x"]).shape[1])
    B = int(np.asarray(inputs["x"]).shape[0])
    assert B == 4
    if S not in _CACHE:
        _CACHE[S] = build(S)
    nc = _CACHE[S]
    maps = prepare_inputs(inputs, S)
    res = run_bass_kernel_spmd(nc, maps, core_ids=list(range(8)))
    NT = S // 2
    out = np.empty((B, S, D), np.float32)
    for core in range(8):
        b, half = core // 2, core % 2
        out[b, half * NT:(half + 1) * NT] = res.results[core]["out"]
    return out
```

```python
import contextlib
import math
import numpy as np
import concourse.bass as bass
import concourse.mybir as mybir
from concourse.bass_utils import run_bass_kernel_spmd

F32 = mybir.dt.float32
BF16 = mybir.dt.bfloat16
AF = mybir.ActivationFunctionType
ALU = mybir.AluOpType

ENGS = ("pe", "act", "dve", "pool", "sp")
D = 1024
CTX = 256
INW = 2816
DFF = 2816
EPS = 1e-6
PAIRS = [[0, 1], [2, 3], [4, 5], [6, 7]]


class Op:
    __slots__ = ("eng", "fn", "waits", "dma", "dsem", "signal", "count", "dinc")

    def __init__(self, eng, fn, dma=False):
        self.eng = eng
        self.fn = fn
        self.waits = []
        self.dma = dma
        self.dsem = None
        self.signal = False
        self.count = 0
        self.dinc = 16


class Prog:
    def __init__(self, nc):
        self.nc = nc
        self.outer = contextlib.ExitStack()
        self.sems = {}
        self.dcum = {}
        self.cnt = {e: 0 for e in ENGS}
        self.waited = {e: {} for e in ENGS}
        self.stack = None
        self.nalloc = 0
        self.hard = False
        self.begin()

    def begin(self):
        self.ops = {e: [] for e in ENGS}
        self.track = {}
        self.stack = contextlib.ExitStack()

    def sbuf(self, name, shape, dtype, outer=False):
        st = self.outer if outer else self.stack
        self.nalloc += 1
        return st.enter_context(self.nc.sbuf_tensor(f"{name}_s{self.nalloc}", list(shape), dtype))

    def psum(self, name, shape, dtype=F32):
        self.nalloc += 1
        return self.stack.enter_context(self.nc.psum_tensor(f"{name}_p{self.nalloc}", list(shape), dtype))

    def sem(self, name):
        if name not in self.sems:
            self.sems[name] = self.outer.enter_context(self.nc.semaphore(name))
        return self.sems[name]

    def _dep(self, op, prod):
        if prod is None or prod is op:
            return
        if prod.dma:
            op.waits.append((prod.dsem, self.dcum[prod.dsem]))
        else:
            if prod.eng == op.eng and (not self.hard or op.eng == "pe"):
                return
            prod.signal = True
            op.waits.append(prod)

    def add(self, eng, fn, reads=(), writes=(), dsem=None, dinc=16):
        dma = dsem is not None
        op = Op(eng, fn, dma)
        for k in reads:
            t = self.track.get(k)
            if t is not None:
                self._dep(op, t[0])
        for k in writes:
            t = self.track.get(k)
            if t is not None:
                self._dep(op, t[0])
                for r in t[1].values():
                    self._dep(op, r)
        if dma:
            op.dsem = dsem
            self.dcum[dsem] = self.dcum.get(dsem, 0) + dinc
            op.dinc = dinc
        for k in reads:
            t = self.track.setdefault(k, [None, {}])
            t[1][eng if not dma else ("dma", dsem)] = op
        for k in writes:
            self.track[k] = [op, {}]
        self.ops[eng].append(op)
        return op

    def end(self, barrier=True):
        nc = self.nc
        for e in ENGS:
            for op in self.ops[e]:
                if op.signal and not op.dma:
                    self.cnt[e] += 1
                    op.count = self.cnt[e]
        esem = {e: self.sem("s_" + e) for e in ENGS}
        for d in self.dcum:
            self.sem(d)
        sems = self.sems

        def run(en):
            def body(eng):
                waited = self.waited[en]
                for op in self.ops[en]:
                    for w in op.waits:
                        if isinstance(w, tuple):
                            sname, val = w
                        else:
                            sname, val = "s_" + w.eng, w.count
                        if waited.get(sname, 0) >= val:
                            continue
                        waited[sname] = val
                        eng.wait_ge(sems[sname], val)
                    ins = op.fn(eng)
                    if op.dma:
                        ins.then_inc(sems[op.dsem], op.dinc)
                    elif op.signal:
                        ins.then_inc(esem[en], 1)
                if en == "sp":
                    for d, c in self.dcum.items():
                        if waited.get(d, 0) < c:
                            waited[d] = c
                            eng.wait_ge(sems[d], c)
            return body

        with nc.Block() as block:
            block.tensor(run("pe"))
            block.scalar(run("act"))
            block.vector(run("dve"))
            block.gpsimd(run("pool"))
            block.sync(run("sp"))
        if barrier:
            nc.all_engine_barrier()
        self.stack.close()
        self.begin()

    def close(self):
        self.stack.close()
        self.outer.close()


def build(S, nlayers=2, debug=False, stop_after=None, gpad=None):
    NT = S // 2
    NG = NT // 512
    NTX = NT + CTX
    KT = S // 128
    NTT = NT // 128
    NB = NTT + 4
    NSET = 1 if NG == 1 else 3
    nc = bass.Bass("TRN2", target_bir_lowering=False)

    def din(name, shape, dt=F32):
        return nc.dram_tensor(name, list(shape), dt, kind="ExternalInput").ap()

    x_in = din("x", [NT, D])
    ctx_in = din("ctx", [CTX, D])
    cvec = din("cvec", [16, 128])
    vecs = din("vecs", [nlayers, 128, 128])
    lamv = din("lamv", [nlayers, 1, 256])
    vg_in = din("vg", [nlayers, 1, 256])
    w_mod = din("w_mod", [nlayers, D, 6 * D])
    w_in = din("w_in", [nlayers, D, INW])
    w_out = din("w_out", [nlayers, D, D])
    w1 = din("w1", [nlayers, D, DFF])
    w3 = din("w3", [nlayers, D, DFF])
    w2 = din("w2", [nlayers, DFF, D])
    wsT_in = din("wsT", [nlayers, 4, 128, 128])
    cos_in = din("cos", [128, NT])
    sin_in = din("sin", [128, NT])
    nab = din("nab", [nlayers, 4, NSET * 8, 128, 512])
    ident_in = din("ident", [128, 128])
    rperm_in = din("rperm", [128, 128])
    blk_in = din("blk", [128, 128])
    out = nc.dram_tensor("out", [NT, D], F32, kind="ExternalOutput").ap()

    def scratch(name, shape, dt):
        if debug:
            return nc.dram_tensor(name, list(shape), dt, kind="ExternalOutput")
        return nc.dram_tensor(name, list(shape), dt)

    xT = scratch("xT", [D, NTX], F32).ap()
    qT = scratch("qT", [768, NTX], BF16).ap()
    kT_in_t = [nc.dram_tensor(f"kT_in{g}", [768, 512], BF16) for g in range(NG)]
    kT_all_t = [nc.dram_tensor(f"kT_all{g}", [2 * 768, 512], BF16) for g in range(NG)]
    V_in_t = [nc.dram_tensor(f"V_in{g}", [512, 768], BF16) for g in range(NG)]
    V_all_t = [nc.dram_tensor(f"V_all{g}", [2 * 512, 768], BF16) for g in range(NG)]
    kT_in = [t.ap() for t in kT_in_t]
    kT_all = [t.ap() for t in kT_all_t]
    V_in = [t.ap() for t in V_in_t]
    V_all = [t.ap() for t in V_all_t]
    kTc = scratch("kTc", [768, CTX], BF16).ap()
    Vc = scratch("Vc", [CTX, 768], BF16).ap()
    mixT = scratch("mixT", [D, NTX], BF16).ap()
    h2T = scratch("h2T", [D, NTX], BF16).ap()
    gT = scratch("gT", [DFF, NTX], BF16).ap()
    if debug:
        dbg_k = nc.dram_tensor("dbg_k", [2 * 768, 512], BF16, kind="ExternalOutput").ap()
        dbg_v = nc.dram_tensor("dbg_v", [2 * 512, 768], BF16, kind="ExternalOutput").ap()

    P = Prog(nc)
    A = P.add

    identf = P.sbuf("identf", [128, 128], F32, outer=True)
    identb = P.sbuf("identb", [128, 128], BF16, outer=True)
    rperm = P.sbuf("rperm", [128, 128], BF16, outer=True)
    blk = P.sbuf("blk", [128, 128], BF16, outer=True)
    onesb = P.sbuf("onesb", [128, 128], BF16, outer=True)
    onesf = P.sbuf("onesf", [128, 128], F32, outer=True)
    vT = P.sbuf("vT", [128, 128], F32, outer=True)
    modT = P.sbuf("modT", [128, 48, 2], F32, outer=True)
    A1 = P.sbuf("A1", [128, 8, 2], F32, outer=True)
    A2 = P.sbuf("A2", [128, 8, 2], F32, outer=True)
    sgl = P.sbuf("sgl", [128, 1], F32, outer=True)
    nlam = P.sbuf("nlam", [128, 1], F32, outer=True)
    vgB = P.sbuf("vgB", [128, 256], F32, outer=True)
    scT = P.sbuf("scT", [128, 8, 2], F32, outer=True)

    groups = [(g, 512, g * 512, False) for g in range(NG)] + [(NG, CTX, NT, True)]

    def mm(o, l, r, st, sp, rd, wr):
        A("pe", lambda e: e.matmul(o, lhsT=l, rhs=r, start=st, stop=sp), rd, wr)

    def tr(o, i, idn, rd, wr):
        A("pe", lambda e: e.transpose(out=o, in_=i, identity=idn), rd, wr)

    def act(o, i, f, rd, wr, scale=1.0, bias=0.0, eng="act"):
        A(eng, lambda e: e.activation(out=o, in_=i, func=f, scale=scale, bias=bias), rd, wr)

    def cp(eng, o, i, rd, wr):
        if eng == "act":
            A("act", lambda e: e.copy(out=o, in_=i), rd, wr)
        else:
            A(eng, lambda e: e.tensor_copy(out=o, in_=i), rd, wr)

    def tt(eng, o, a, b, op, rd, wr):
        A(eng, lambda e: e.tensor_tensor(out=o, in0=a, in1=b, op=op), rd, wr)

    def ts(eng, o, a, s1, s2, op0, op1, rd, wr):
        if s2 is None:
            A(eng, lambda e: e.tensor_scalar(out=o, in0=a, scalar1=s1, scalar2=None, op0=op0), rd, wr)
        else:
            A(eng, lambda e: e.tensor_scalar(out=o, in0=a, scalar1=s1, scalar2=s2, op0=op0, op1=op1), rd, wr)

    def stt(o, a, sc, b, op0, op1, rd, wr):
        A("dve", lambda e: e.scalar_tensor_tensor(out=o, in0=a, scalar=sc, in1=b, op0=op0, op1=op1), rd, wr)

    def dma(o, i, rd, wr, sem, eng="sp"):
        A(eng, lambda e: e.dma_start(out=o, in_=i), rd, wr, dsem=sem)

    def rstd_from_sum(o, ps, n, rd, wr, tmp):
        act(tmp, ps, AF.Ln, rd, [wr + "_t"], scale=1.0 / n, bias=EPS)
        act(o, tmp, AF.Exp, [wr + "_t"], [wr], scale=-0.5)

    P.hard = True
    st32 = P.sbuf("c_st", [128, 3, 128], F32)
    dma(identf[:], ident_in, [], ["identf"], "d_c0")
    dma(st32[:, 0, :], ident_in, [], ["st0"], "d_c1")
    dma(st32[:, 1, :], rperm_in, [], ["st1"], "d_c1")
    dma(st32[:, 2, :], blk_in, [], ["st2"], "d_c1")
    cp("dve", identb[:], st32[:, 0, :], ["st0"], ["identb"])
    cp("dve", rperm[:], st32[:, 1, :], ["st1"], ["rperm"])
    cp("dve", blk[:], st32[:, 2, :], ["st2"], ["blk"])
    A("dve", lambda e: e.memset(onesb[:], 1.0), [], ["onesb"])
    A("dve", lambda e: e.memset(onesf[:], 1.0), [], ["onesf"])
    cv = P.sbuf("cv", [16, 128], F32)
    ps_c = P.psum("ps_c", [128, 16])
    dma(cv[:], cvec, [], ["cv"], "d_c2")
    tr(ps_c[:, :], cv[:], identf[0:16, 0:16], ["cv", "identf"], ["ps_c"])
    act(scT[:].rearrange("p k s -> p s k"), ps_c[:].rearrange("p (s k) -> p s k", s=2), AF.Silu, ["ps_c"], ["scT"])
    P.end()
    P.hard = False

    xin = [P.sbuf(f"xin{i}", [128, D], F32) for i in range(2)]
    xg_t = [P.sbuf(f"xgT{i}", [128, 8, 512], F32) for i in range(2)]
    ps_t = [P.psum(f"ps_t{i}", [128, 512]) for i in range(4)]
    n = 0
    for (g, W, off, isctx) in groups:
        xb = xg_t[g % 2]
        for t4 in range(W // 128):
            src = ctx_in[t4 * 128:(t4 + 1) * 128, :] if isctx else x_in[off + t4 * 128: off + (t4 + 1) * 128, :]
            xi = xin[n % 2]
            dma(xi[:], src, [], [f"xin{n % 2}"], f"d_xin{n % 2}")
            for b in range(2):
                pb = ps_t[(n % 2) * 2 + b]
                for c4 in range(4):
                    c = b * 4 + c4
                    tr(pb[:, c4 * 128:(c4 + 1) * 128], xi[:, c * 128:(c + 1) * 128], identf[:], [f"xin{n % 2}", "identf"], [f"ps_t{(n % 2) * 2 + b}"])
                cp("act" if b == 0 else "dve", xb[:, b * 4:(b + 1) * 4, t4 * 128:(t4 + 1) * 128], pb[:].rearrange("p (c n) -> p c n", c=4),
                   [f"ps_t{(n % 2) * 2 + b}"], [f"xgT{g % 2}"])
            n += 1
        dma(xT[:, off:off + W].rearrange("(c p) n -> p c n", p=128), xb[:, :, 0:W], [f"xgT{g % 2}"], [("xT", g)], f"d_xgT{g % 2}")
    P.end()
    if stop_after == "T":
        P.close()
        return nc

    for l in range(nlayers):
        last = (l == nlayers - 1)
        lam_init = 0.8 - 0.6 * math.exp(-0.3 * l)
        act_groups = groups[:NG] if last else groups

        P.hard = True
        vraw = P.sbuf("vraw", [128, 128], F32)
        ps_v = P.psum("ps_v", [128, 128])
        dma(vraw[:], vecs[l], [], ["vraw"], "d_m0")
        tr(ps_v[:, :], vraw[:], identf[:], ["vraw"], ["ps_v"])
        cp("act", vT[:], ps_v[:], ["ps_v"], ["vT"])
        ts("dve", sgl[:], vT[:, 68:69], 1.0 - lam_init, None, ALU.mult, None, ["vT"], ["sgl"])
        lv = P.sbuf("lv", [1, 256], F32)
        lacc = P.sbuf("lacc", [1, 32], F32)
        lacc0 = P.sbuf("lacc0", [1, 4], F32)
        A("dve", lambda e: e.memset(lacc[:], 0.0), [], ["lacc"])
        ljunk = P.sbuf("ljunk", [1, 64], F32)
        ps_l = P.psum("ps_l", [128, 2])
        dma(lv[:], lamv[l], [], ["lv"], "d_m1")
        for i in range(2):
            tt("dve", ljunk[:], lv[:, (2 * i) * 64:(2 * i + 1) * 64], lv[:, (2 * i + 1) * 64:(2 * i + 2) * 64], ALU.mult, ["lv"], ["ljunk"])
            A("dve", lambda e, i=i: e.tensor_reduce(out=lacc0[:, i:i + 1], in_=ljunk[:], axis=mybir.AxisListType.X, op=ALU.add), ["ljunk"], ["lacc0"])
        cp("dve", lacc[:, 0:2], lacc0[:, 0:2], ["lacc0"], ["lacc"])
        act(lacc[:, 0:32], lacc[:, 0:32], AF.Exp, ["lacc"], ["lacc"])
        tt("dve", lacc[:, 2:3], lacc[:, 1:2], lacc[:, 0:1], ALU.subtract, ["lacc"], ["lacc"])
        ts("dve", lacc[:, 2:4], lacc[:, 2:4], 1.0, -lam_init, ALU.mult, ALU.add, ["lacc"], ["lacc"])
        mm(ps_l[:, 0:2], onesf[0:1, :], lacc[:, 2:4], True, True, ["lacc", "onesf"], ["ps_l"])
        cp("act", nlam[:], ps_l[:, 0:1], ["ps_l"], ["nlam"])
        vgr = P.sbuf("vgr", [1, 256], F32)
        ps_g = P.psum("ps_g", [128, 256])
        dma(vgr[:], vg_in[l], [], ["vgr"], "d_m2")
        mm(ps_g[:, :], onesf[0:1, :], vgr[:], True, True, ["vgr", "onesf"], ["ps_g"])
        cp("act", vgB[:], ps_g[:], ["ps_g"], ["vgB"])
        wm = [P.sbuf(f"wm{i}", [128, 8, 1024], F32) for i in range(2)]
        ps_m = P.psum("ps_m", [128, 96])
        for j in range(6):
            wb_ = wm[j % 2]
            for hh in range(2):
                dma(wb_[:, hh * 4:(hh + 1) * 4, :], w_mod[l][hh * 512:(hh + 1) * 512, j * 1024:(j + 1) * 1024].rearrange("(c p) n -> p c n", p=128),
                    [], [f"wm{j % 2}"], f"d_wm{j % 2}")
            for c in range(8):
                ct = j * 8 + c
                for kc in range(8):
                    mm(ps_m[:, ct * 2:ct * 2 + 2], wb_[:, kc, c * 128:(c + 1) * 128], scT[:, kc, :], kc == 0, kc == 7, [f"wm{j % 2}", "scT"], ["ps_m"])
        for s in range(2):
            tt("dve", modT[:, :, s], ps_m[:].rearrange("p (c s) -> p c s", s=2)[:, :, s], vT[:, 0:48], ALU.add, ["ps_m", "vT"], ["modT"])
        for s in range(2):
            stt(A1[:, :, s], modT[:, 8:16, s], 1.0, vT[:, 48:56], ALU.add, ALU.mult, ["modT", "vT"], ["A1"])
            stt(A2[:, :, s], modT[:, 32:40, s], 1.0, vT[:, 56:64], ALU.add, ALU.mult, ["modT", "vT"], ["A2"])
        P.end()
        P.hard = False

        def Bv(j, c, s):
            return modT[:, j * 8 + c, s:s + 1]

        def norm_mod(xg, W, Aq, jshift, s, hT, sq, rs, tmpb, ps_st, key, pskey, rst):
            for c in range(8):
                tt("pool", sq[:, c, 0:W], xg[:, c, 0:W], xg[:, c, 0:W], ALU.mult, [key], ["sq"])
            for c in range(8):
                mm(ps_st[:, 0:W], onesb[:], sq[:, c, 0:W], c == 0, c == 7, ["sq", "onesb"], [pskey])
            rstd_from_sum(rs[:, 0:W], ps_st[:, 0:W], float(D), [pskey], "rs", rst[:, 0:W])
            for c in range(8):
                tb = tmpb[c % 2]
                tt("dve", tb[:, 0:W], xg[:, c, 0:W], rs[:, 0:W], ALU.mult, [key, "rs"], [f"tmpb{c % 2}"])
                act(hT[:, c, 0:W], tb[:, 0:W], AF.Identity, [f"tmpb{c % 2}"], ["hT"], scale=Aq[:, c, s:s + 1], bias=Bv(jshift, c, s))

        wi = P.sbuf("wi", [128, 8, INW], BF16)
        for hh in range(2):
            for c in range(8):
                dma(wi[:, c, hh * 1408:(hh + 1) * 1408], w_in[l][c * 128:(c + 1) * 128, hh * 1408:(hh + 1) * 1408], [], [("wi", hh)], f"d_wi{hh}", eng="pool")
        wsr = P.sbuf("wsr", [128, 4, 128], BF16)
        dma(wsr[:], wsT_in[l].rearrange("g s t -> s g t"), [], ["wsr"], "d_ws", eng="pool")
        xg_p = [P.sbuf(f"xg{i}", [128, 8, 512], F32) for i in range(2)]
        cs_p = [P.sbuf(f"cs{i}", [128, 2, 512], F32) for i in range(2)]
        sq = P.sbuf("sq", [128, 8, 512], BF16)
        rs = P.sbuf("rs", [128, 512], F32)
        tmpb = [P.sbuf(f"tmpb{i}", [128, 512], F32) for i in range(2)]
        hT = P.sbuf("hT", [128, 8, 512], BF16)
        qf = [P.sbuf(f"qf{i}", [128, 512], F32) for i in range(2)]
        sqq_ = [P.sbuf(f"sqq{i}", [128, 512], BF16) for i in range(2)]
        lnq_ = [P.sbuf(f"lnq{i}", [128, 512], F32) for i in range(2)]
        rr_ = [P.sbuf(f"rr{i}", [128, 512], F32) for i in range(2)]
        qn_ = [P.sbuf(f"qn{i}", [128, 512], F32) for i in range(2)]
        qnb_ = [P.sbuf(f"qnb{i}", [128, 512], BF16) for i in range(2)]
        t1_ = [P.sbuf(f"t1{i}", [128, 512], F32) for i in range(2)]
        t2_ = [P.sbuf(f"t2{i}", [128, 512], F32) for i in range(2)]
        qo = [P.sbuf(f"qo{i}", [128, 512], BF16) for i in range(4)]
        z_ = [P.sbuf(f"z{i}", [128, 512], F32) for i in range(2)]
        zjunk_ = [P.sbuf(f"zjunk{i}", [128, 256], F32) for i in range(2)]
        ssv_ = [P.sbuf(f"ssv{i}", [128, 32], F32) for i in range(2)]
        ssw_ = [P.sbuf(f"ssw{i}", [128, 32], F32) for i in range(2)]
        ssv0_ = [P.sbuf(f"ssv0{i}", [128, 2], F32) for i in range(2)]
        for i in range(2):
            A("dve", lambda e, i=i: e.memset(ssv_[i][:], 1.0), [], [f"ssv{i}"])
        vnb_ = [P.sbuf(f"vnb{i}", [128, 256], BF16) for i in range(2)]
        ob_ = [P.sbuf(f"ob{i}", [128, 256], BF16) for i in range(2)]
        obT = P.sbuf("obT", [128, 2, 512], BF16)
        vt = [P.sbuf(f"vt{i}", [128, 768], BF16) for i in range(2)]
        rst = P.sbuf("rst", [128, 512], F32)
        ps_q = [P.psum(f"ps_q{i}", [128, 512]) for i in range(2)]
        ps_n = P.psum("ps_n", [128, 512])
        ps_r = P.psum("ps_r", [128, 512])
        ps_uv = P.psum("ps_uv", [128, 512])
        ps_va = P.psum("ps_va", [128, 512])
        ps_vc = P.psum("ps_vc", [128, 512])
        ps_tb = P.psum("ps_tb", [128, 256], BF16)

        ftiles = [(0, "a", 64, "q", 0), (128, "a", 64, "q", 1)] + [(256 + 128 * h, "c", 66, "q", 2 + h) for h in range(4)] + \
                 [(1280, "a", 65, "k", 0), (1408, "a", 65, "k", 1)] + [(1792 + 128 * h, "c", 67, "k", 2 + h) for h in range(4)]

        def load_group(gi):
            g, W, off, isctx = groups[gi]
            b = gi % 2
            dma(xg_p[b][:, :, 0:W], xT[:, off:off + W].rearrange("(c p) n -> p c n", p=128), [("xT", g)], [f"xg{b}"], f"d_xg{b}")
            if not isctx:
                dma(cs_p[b][:, 0, :], cos_in[:, off:off + W], [], [f"cs{b}"], f"d_cs{b}")
                dma(cs_p[b][:, 1, :], sin_in[:, off:off + W], [], [f"cs{b}"], f"d_cs{b}")

        def gather(g):
            A("pool", lambda e: e.collective_compute("AllGather", ALU.bypass, replica_groups=PAIRS, ins=[kT_in_t[g].ap().opt()], outs=[kT_all_t[g].ap().opt()]),
              [("kT_in", g)], [("kT_all", g)], dsem="d_cc", dinc=1)
            A("pool", lambda e: e.collective_compute("AllGather", ALU.bypass, replica_groups=PAIRS, ins=[V_in_t[g].ap().opt()], outs=[V_all_t[g].ap().opt()]),
              [("V_in", g)], [("V_all", g)], dsem="d_cc", dinc=1)

        load_group(0)
        nq = 0
        nv = 0
        for gi, (g, W, off, isctx) in enumerate(groups):
            if gi + 1 < len(groups):
                load_group(gi + 1)
            b = gi % 2
            xg = xg_p[b]
            s = 1 if isctx else 0
            norm_mod(xg, W, A1, 0, s, hT, sq, rs, tmpb, ps_r, f"xg{b}", "ps_r", rst)
            atiles = [ft for ft in ftiles if not (isctx and last and ft[3] == "q")]

            def proj(i):
                co = atiles[i][0]
                k = (nq + i) % 2
                for c in range(8):
                    mm(ps_q[k][:, 0:W], wi[:, c, co:co + 128], hT[:, c, 0:W], c == 0, c == 7, [("wi", 0 if co < 1408 else 1), "hT"], [f"ps_q{k}"])

            proj(0)
            nq0 = nq
            for ti, (co, kind, gcol, dst, dti) in enumerate(atiles):
                nqi = nq0 + ti
                k2 = nqi % 2
                pq = ps_q[k2]
                pqk = f"ps_q{k2}"
                qfb = qf[k2]
                qfk = f"qf{k2}"
                act(sqq_[k2][:, 0:W], pq[:, 0:W], AF.Square, [pqk], [f"sqq{k2}"])
                if ti + 1 < len(atiles):
                    proj(ti + 1)
                mm(ps_n[:, 0:W], blk[:], sqq_[k2][:, 0:W], True, True, [f"sqq{k2}", "blk"], ["ps_n"])
                act(lnq_[k2][:, 0:W], ps_n[:, 0:W], AF.Ln, ["ps_n"], [f"lnq{k2}"], scale=1.0 / 64, bias=EPS)
                act(rr_[k2][:, 0:W], lnq_[k2][:, 0:W], AF.Exp, [f"lnq{k2}"], [f"rr{k2}"], scale=-0.5)
                qob = qo[nqi % 4]
                qok = f"qo{nqi % 4}"
                rope = (kind == "c") and not isctx
                if not rope:
                    stt(qob[:, 0:W], pq[:, 0:W], vT[:, gcol:gcol + 1], rr_[k2][:, 0:W], ALU.mult, ALU.mult, [pqk, f"rr{k2}", "vT"], [qok])
                else:
                    stt(qn_[k2][:, 0:W], pq[:, 0:W], vT[:, gcol:gcol + 1], rr_[k2][:, 0:W], ALU.mult, ALU.mult, [pqk, f"rr{k2}", "vT"], [f"qn{k2}"])
                    cp("act", qnb_[k2][:, 0:W], qn_[k2][:, 0:W], [f"qn{k2}"], [f"qnb{k2}"])
                    mm(ps_r[:, 0:W], rperm[:], qnb_[k2][:, 0:W], True, True, [f"qnb{k2}", "rperm"], ["ps_r"])
                    tt("pool", t1_[k2][:, 0:W], qn_[k2][:, 0:W], cs_p[b][:, 0, 0:W], ALU.mult, [f"qn{k2}", f"cs{b}"], [f"t1{k2}"])
                    tt("dve", t2_[k2][:, 0:W], ps_r[:, 0:W], cs_p[b][:, 1, 0:W], ALU.mult, ["ps_r", f"cs{b}"], [f"t2{k2}"])
                    tt("dve", qob[:, 0:W], t1_[k2][:, 0:W], t2_[k2][:, 0:W], ALU.add, [f"t1{k2}", f"t2{k2}"], [qok])
                if dst == "q":
                    dma(qT[dti * 128:(dti + 1) * 128, off:off + W], qob[:, 0:W], [qok], [("qT", g)], f"d_qo{nqi % 4}")
                elif isctx:
                    dma(kTc[dti * 128:(dti + 1) * 128, :], qob[:, 0:W], [qok], ["kTc"], f"d_qo{nqi % 4}")
                else:
                    dma(kT_in[g][dti * 128:(dti + 1) * 128, :], qob[:, 0:W], [qok], [("kT_in", g)], f"d_qo{nqi % 4}")
            nq = nq0 + len(atiles)
            gate_on = not (isctx and last)
            nv0 = nv

            def stage_a(t4):
                tsl = slice(t4 * 128, (t4 + 1) * 128)
                kv = (nv0 + t4) % 2
                vtb, vtk = vt[kv], f"vt{kv}"
                for c in range(8):
                    mm(ps_va[:, 0:256], hT[:, c, tsl], wi[:, c, 1536:1792], c == 0, c == 7, [("wi", 1), "hT"], ["ps_va"])
                for c in range(8):
                    mm(ps_vc[:, :], hT[:, c, tsl], wi[:, c, 2304:2816], c == 0, c == 7, [("wi", 1), "hT"], ["ps_vc"])
                cp("act", vtb[:, 0:256], ps_va[:, 0:256], ["ps_va"], [vtk])
                cp("act", vtb[:, 256:768], ps_vc[:, :], ["ps_vc"], [vtk])
                if isctx:
                    dma(Vc[tsl, :], vtb[:], [vtk], ["Vc"], f"d_vt{kv}")
                else:
                    dma(V_in[g][t4 * 128:(t4 + 1) * 128, :], vtb[:], [vtk], [("V_in", g)], f"d_vt{kv}")
                if not gate_on:
                    return
                for c in range(8):
                    mm(ps_uv[:, :], hT[:, c, tsl], wi[:, c, 768:1280], c == 0, c == 7, [("wi", 0), "hT"], ["ps_uv"])
                act(z_[kv][:], ps_uv[:], AF.Gelu_apprx_tanh, ["ps_uv"], [f"z{kv}"])
                tt("pool", zjunk_[kv][:], z_[kv][:, 256:512], z_[kv][:, 256:512], ALU.mult, [f"z{kv}"], [f"zjunk{kv}"])
                A("dve", lambda e, k=kv: e.tensor_reduce(out=ssv0_[k][:, 0:1], in_=zjunk_[k][:], axis=mybir.AxisListType.X, op=ALU.add), [f"zjunk{kv}"], [f"ssv0{kv}"])
                P.hard = True
                cp("dve", ssv_[kv][:, 0:1], ssv0_[kv][:, 0:1], [f"ssv0{kv}"], [f"ssv{kv}"])
                act(ssw_[kv][:, 0:32], ssv_[kv][:, 0:32], AF.Ln, [f"ssv{kv}"], [f"ssw{kv}"], scale=1.0 / 256, bias=EPS)
                act(ssw_[kv][:, 0:32], ssw_[kv][:, 0:32], AF.Exp, [f"ssw{kv}"], [f"ssw{kv}"], scale=-0.5)
                P.hard = False
                stt(vnb_[kv][:], z_[kv][:, 256:512], ssw_[kv][:, 0:1], vgB[:], ALU.mult, ALU.mult, [f"z{kv}", f"ssw{kv}", "vgB"], [f"vnb{kv}"])

            def stage_b(t4):
                tsl = slice(t4 * 128, (t4 + 1) * 128)
                kv = (nv0 + t4) % 2
                for gg in range(4):
                    mm(ps_va[:, 256 + gg * 64:256 + (gg + 1) * 64], wsr[:, gg, :], vnb_[kv][:, gg * 64:(gg + 1) * 64], True, True, ["wsr", f"vnb{kv}"], ["ps_va"])
                for gg in range(4):
                    stt(ob_[kv][:, gg * 64:(gg + 1) * 64], ps_va[:, 256 + gg * 64:256 + (gg + 1) * 64], vT[:, 69 + gg:70 + gg], z_[kv][:, gg * 64:(gg + 1) * 64],
                        ALU.add, ALU.mult, ["ps_va", f"z{kv}", "vT"], [f"ob{kv}"])
                for j in range(2):
                    tr(ps_tb[:, j * 128:(j + 1) * 128], ob_[kv][:, j * 128:(j + 1) * 128], identb[:], [f"ob{kv}", "identb"], ["ps_tb"])
                cp("act", obT[:, :, tsl], ps_tb[:].rearrange("p (j n) -> p j n", j=2), ["ps_tb"], ["obT"])

            ntt = W // 128
            stage_a(0)
            for t4 in range(ntt):
                if t4 + 1 < ntt:
                    stage_a(t4 + 1)
                if gate_on:
                    stage_b(t4)
            nv = nv0 + ntt
            if not (isctx and last):
                dma(mixT[256:512, off:off + W].rearrange("(j p) n -> p j n", p=128), obT[:, :, 0:W], ["obT"], [("mixT", g)], "d_obT")
            if gi >= 1:
                gather(gi - 1)
        if debug and l == 0:
            dma(dbg_k, kT_all[0], [("kT_all", 0)], [], "d_dbg")
            dma(dbg_v, V_all[0], [("V_all", 0)], [], "d_dbg")
        P.end()
        if stop_after == f"P{l}":
            P.close()
            return nc

        kb = P.sbuf("kb", [128, (NB + 2) * 128], BF16)
        vb = P.sbuf("vb", [128, NB + 2, 128], BF16)
        tbl = [P.sbuf(f"tbl{i}", [128, NSET * 8, 512], F32) for i in range(2)]
        qa = [P.sbuf(f"qa{i}", [128, 512], BF16) for i in range(2)]
        sb = [P.sbuf(f"sbias{i}", [128, 512], F32) for i in range(4)]
        pt = [P.sbuf(f"pt{i}", [128, 512], BF16) for i in range(4)]
        rcp = P.sbuf("rcp", [128, 512], F32)
        oa = [P.sbuf(f"oa{i}", [128, 512], BF16) for i in range(2)]
        ps_s = [P.psum(f"ps_s{i}", [128, 512]) for i in range(4)]
        ps_o = [P.psum(f"ps_o{i}", [128, 512]) for i in range(2)]
        ps_d = [P.psum(f"ps_d{i}", [128, 512]) for i in range(2)]
        nql = 0
        nt = 0
        for t in range(2):
            dma(kb[:, 0:256], kT_all[NG - 1][t * 128:(t + 1) * 128, 256:512], [], ["kb"], "d_kb")
            for g2 in range(NG):
                dma(kb[:, 256 + g2 * 512:256 + (g2 + 1) * 512], kT_in[g2][t * 128:(t + 1) * 128, :], [], ["kb"], "d_kb")
            dma(kb[:, 256 + NT:512 + NT], kT_all[0][768 + t * 128:768 + (t + 1) * 128, 0:256], [], ["kb"], "d_kb")
            dma(kb[:, NB * 128:(NB + 2) * 128], kTc[t * 128:(t + 1) * 128, :], [], ["kb"], "d_kb")
            dma(vb[:, 0:2, :], V_all[NG - 1][256:512, t * 128:(t + 1) * 128].rearrange("(k p) c -> p k c", p=128), [], ["vb"], "d_vb")
            for g2 in range(NG):
                dma(vb[:, 2 + g2 * 4:2 + (g2 + 1) * 4, :], V_in[g2][:, t * 128:(t + 1) * 128].rearrange("(k p) c -> p k c", p=128), [], ["vb"], "d_vb")
            dma(vb[:, 2 + NTT:NB, :], V_all[0][512:768, t * 128:(t + 1) * 128].rearrange("(k p) c -> p k c", p=128), [], ["vb"], "d_vb")
            dma(vb[:, NB:NB + 2, :], Vc[:, t * 128:(t + 1) * 128].rearrange("(k p) c -> p k c", p=128), [], ["vb"], "d_vb")
            for hh in range(2):
                dma(tbl[hh][:], nab[l, 2 * t + hh].rearrange("v p n -> p v n"), [], [f"tbl{hh}"], f"d_tbl{hh}")
            for (g, W, off, isctx) in act_groups:
                qab = qa[nql % 2]
                qak = f"qa{nql % 2}"
                dma(qab[:, 0:W], qT[t * 128:(t + 1) * 128, off:off + W], [("qT", g)], [qak], f"d_qa{nql % 2}")
                pso, psd = ps_o[nql % 2], ps_d[nql % 2]
                pok, pdk = f"ps_o{nql % 2}", f"ps_d{nql % 2}"
                for hh in range(2):
                    pl = slice(hh * 64, (hh + 1) * 64)
                    if isctx:
                        tiles = [(NB, None), (NB + 1, None)]
                    else:
                        vset = 0 if NSET == 1 else (0 if g == 0 else (2 if g == NG - 1 else 1))
                        tiles = [(g * 4 + i, vset * 8 + i) for i in range(8)] + [(NB, None), (NB + 1, None)]
                    base = nt

                    def smm_na(idx):
                        bt, var = tiles[idx]
                        k = (base + idx) % 4
                        mm(ps_s[k][:, 0:W], kb[pl, bt * 128:(bt + 1) * 128], qab[pl, 0:W], True, True, ["kb", qak], [f"ps_s{k}"])

                    for j0 in range(min(2, len(tiles))):
                        smm_na(j0)
                    for idx, (bt, var) in enumerate(tiles):
                        if idx + 2 < len(tiles):
                            smm_na(idx + 2)
                        k = (base + idx) % 4
                        pss, psk, ptb, ptk = ps_s[k], f"ps_s{k}", pt[k], f"pt{k}"
                        if var is None:
                            act(ptb[:, 0:W], pss[:, 0:W], AF.Exp, [psk], [ptk], scale=0.125)
                        else:
                            sbb = sb[k]
                            stt(sbb[:, 0:W], pss[:, 0:W], 0.125, tbl[hh][:, var, 0:W], ALU.mult, ALU.add, [psk, f"tbl{hh}"], [f"sbias{k}"])
                            act(ptb[:, 0:W], sbb[:, 0:W], AF.Exp, [f"sbias{k}"], [ptk])
                        first, lastt = idx == 0, idx == len(tiles) - 1
                        mm(pso[pl, 0:W], vb[:, bt, hh * 64:(hh + 1) * 64], ptb[:, 0:W], first, lastt, ["vb", ptk], [pok])
                        mm(psd[pl, 0:W], onesb[:, 0:64], ptb[:, 0:W], first, lastt, ["onesb", ptk], [pdk])
                        nt += 1
                A("dve", lambda e, psd=psd, W=W: e.reciprocal(out=rcp[:, 0:W], in_=psd[:, 0:W]), [pdk], ["rcp"])
                oab = oa[nql % 2]
                tt("dve", oab[:, 0:W], pso[:, 0:W], rcp[:, 0:W], ALU.mult, [pok, "rcp"], [f"oa{nql % 2}"])
                dma(mixT[t * 128:(t + 1) * 128, off:off + W], oab[:, 0:W], [f"oa{nql % 2}"], [("mixT", g)], f"d_oa{nql % 2}")
                nql += 1
        P.end()
        if stop_after == f"NA{l}":
            P.close()
            return nc

        kd = [P.sbuf(f"kd{i}", [128, (KT + 2) * 128], BF16) for i in range(2)]
        vd = [P.sbuf(f"vd{i}", [128, KT + 2, 128], BF16) for i in range(2)]
        qd = [P.sbuf(f"qd{i}", [128, 512], BF16) for i in range(2)]
        ptd = [P.sbuf(f"ptd{i}", [128, 2, 512], BF16) for i in range(3)]
        r12 = P.sbuf("r12", [128, 2, 512], F32)
        o1 = P.sbuf("o1", [128, 512], F32)
        o2 = P.sbuf("o2", [128, 512], F32)
        osq = P.sbuf("osq", [128, 512], BF16)
        lno = P.sbuf("lno", [128, 512], F32)
        oc = [P.sbuf(f"oc{i}", [128, 512], BF16) for i in range(2)]
        ps_sd = [P.psum(f"ps_sd{i}", [128, 2, 512]) for i in range(2)]
        ps_od = [P.psum(f"ps_od{m}", [128, 512]) for m in range(2)]
        ps_dd = P.psum("ps_dd", [128, 2, 512])
        dacc = P.sbuf("dacc", [128, 2, 512], F32)

        def load_head(h):
            b = h % 2
            for r in range(2):
                for g2 in range(NG):
                    dma(kd[b][:, (r * NG + g2) * 512:(r * NG + g2 + 1) * 512], kT_all[g2][r * 768 + (2 + h) * 128: r * 768 + (3 + h) * 128, :], [], [f"kd{b}"], f"d_kd{b}")
            dma(kd[b][:, KT * 128:(KT + 2) * 128], kTc[(2 + h) * 128:(3 + h) * 128, :], [], [f"kd{b}"], f"d_kd{b}")
            for r in range(2):
                for g2 in range(NG):
                    dma(vd[b][:, (r * NG + g2) * 4:(r * NG + g2 + 1) * 4, :], V_all[g2][r * 512:(r + 1) * 512, 256 + h * 128:256 + (h + 1) * 128].rearrange("(k p) c -> p k c", p=128),
                        [], [f"vd{b}"], f"d_vd{b}")
            dma(vd[b][:, KT:KT + 2, :], Vc[:, 256 + h * 128:256 + (h + 1) * 128].rearrange("(k p) c -> p k c", p=128), [], [f"vd{b}"], f"d_vd{b}")

        load_head(0)
        nql = 0
        for h in range(4):
            if h + 1 < 4:
                load_head(h + 1)
            b = h % 2
            kdb, vdb = kd[b], vd[b]
            for (g, W, off, isctx) in act_groups:
                qdb = qd[nql % 2]
                qdk = f"qd{nql % 2}"
                dma(qdb[:, 0:W], qT[(2 + h) * 128:(3 + h) * 128, off:off + W], [("qT", g)], [qdk], f"d_qd{nql % 2}")
                ktiles = [KT, KT + 1] if isctx else list(range(KT + 2))

                def smm(idx):
                    kt = ktiles[idx]
                    for m in range(2):
                        pl = slice(m * 64, (m + 1) * 64)
                        mm(ps_sd[idx % 2][:, m, 0:W], kdb[pl, kt * 128:(kt + 1) * 128], qdb[pl, 0:W], True, True, [f"kd{b}", qdk], [f"ps_sd{idx % 2}"])

                smm(0)
                for idx, kt in enumerate(ktiles):
                    if idx + 1 < len(ktiles):
                        smm(idx + 1)
                    first, lastt = idx == 0, idx == len(ktiles) - 1
                    ptb = ptd[idx % 3]
                    ptk = f"ptd{idx % 3}"
                    act(ptb[:, :, 0:W], ps_sd[idx % 2][:, :, 0:W], AF.Exp, [f"ps_sd{idx % 2}"], [ptk], scale=0.125)
                    for m in range(2):
                        mm(ps_od[m][:, 0:W], vdb[:, kt, :], ptb[:, m, 0:W], first, lastt, [f"vd{b}", ptk], [f"ps_od{m}"])
                    if first:
                        cp("dve", ps_dd[:, :, 0:W], ptb[:, :, 0:W], [ptk], ["ps_dd"])
                    else:
                        tt("dve", ps_dd[:, :, 0:W], ps_dd[:, :, 0:W], ptb[:, :, 0:W], ALU.add, [ptk, "ps_dd"], ["ps_dd"])
                cp("act", dacc[:, :, 0:W], ps_dd[:, :, 0:W], ["ps_dd"], ["dacc"])
                for m in range(2):
                    mm(ps_dd[:, m, 0:W], onesf[:], dacc[:, m, 0:W], True, True, ["onesf", "dacc"], ["ps_dd"])
                act(r12[:, :, 0:W], ps_dd[:, :, 0:W], AF.Ln, ["ps_dd"], ["r12"])
                act(r12[:, :, 0:W], r12[:, :, 0:W], AF.Exp, ["r12"], ["r12"], scale=-1.0)
                tt("dve", o1[:, 0:W], ps_od[0][:, 0:W], r12[:, 0, 0:W], ALU.mult, ["ps_od0", "r12"], ["o1"])
                tt("dve", o2[:, 0:W], ps_od[1][:, 0:W], r12[:, 1, 0:W], ALU.mult, ["ps_od1", "r12"], ["o2"])
                stt(o1[:, 0:W], o2[:, 0:W], nlam[:, 0:1], o1[:, 0:W], ALU.mult, ALU.add, ["o1", "o2", "nlam"], ["o1"])
                tt("pool", osq[:, 0:W], o1[:, 0:W], o1[:, 0:W], ALU.mult, ["o1"], ["osq"])
                pn = ps_sd[0]
                mm(pn[:, 0, 0:W], onesb[:], osq[:, 0:W], True, True, ["osq", "onesb"], ["ps_sd0"])
                act(lno[:, 0:W], pn[:, 0, 0:W], AF.Ln, ["ps_sd0"], ["lno"], scale=1.0 / 128, bias=EPS)
                act(lno[:, 0:W], lno[:, 0:W], AF.Exp, ["lno"], ["lno"], scale=-0.5)
                ocb = oc[nql % 2]
                stt(ocb[:, 0:W], o1[:, 0:W], sgl[:, 0:1], lno[:, 0:W], ALU.mult, ALU.mult, ["o1", "lno", "sgl"], [f"oc{nql % 2}"])
                dma(mixT[512 + h * 128:512 + (h + 1) * 128, off:off + W], ocb[:, 0:W], [f"oc{nql % 2}"], [("mixT", g)], f"d_oc{nql % 2}")
                nql += 1
        P.end()
        if stop_after == f"DA{l}":
            P.close()
            return nc

        wo = P.sbuf("wo", [128, 8, D], BF16)
        for c in range(8):
            dma(wo[:, c, :], w_out[l][c * 128:(c + 1) * 128, :], [], [("wo", c // 4)], f"d_wo{c // 4}", eng="pool")
        mx = [P.sbuf(f"mx{i}", [128, 8, 512], BF16) for i in range(2)]
        xg_o = [P.sbuf(f"xgo{i}", [128, 8, 512], F32) for i in range(2)]
        sq = P.sbuf("sq", [128, 8, 512], BF16)
        rs = P.sbuf("rs", [128, 512], F32)
        tmpb = [P.sbuf(f"tmpb{i}", [128, 512], F32) for i in range(2)]
        h2 = [P.sbuf(f"h2{i}", [128, 8, 512], BF16) for i in range(2)]
        ps_y = [P.psum(f"ps_y{i}", [128, 512]) for i in range(2)]
        ps_st = P.psum("ps_st", [128, 512])
        rst = P.sbuf("rst", [128, 512], F32)

        def load_o1(gi):
            g, W, off, isctx = act_groups[gi]
            b = gi % 2
            dma(mx[b][:, :, 0:W], mixT[:, off:off + W].rearrange("(c p) n -> p c n", p=128), [("mixT", g)], [f"mx{b}"], f"d_mx{b}")
            dma(xg_o[b][:, :, 0:W], xT[:, off:off + W].rearrange("(c p) n -> p c n", p=128), [("xT", g)], [f"xgo{b}"], f"d_xgo{b}")

        load_o1(0)
        for gi, (g, W, off, isctx) in enumerate(act_groups):
            if gi + 1 < len(act_groups):
                load_o1(gi + 1)
            b = gi % 2
            s = 1 if isctx else 0
            xg = xg_o[b]
            for c in range(8):
                py = ps_y[c % 2]
                for kc in range(8):
                    mm(py[:, 0:W], wo[:, kc, c * 128:(c + 1) * 128], mx[b][:, kc, 0:W], kc == 0, kc == 7, [("wo", kc // 4), f"mx{b}"], [f"ps_y{c % 2}"])
                stt(xg[:, c, 0:W], py[:, 0:W], Bv(2, c, s), xg[:, c, 0:W], ALU.mult, ALU.add, [f"ps_y{c % 2}", f"xgo{b}", "modT"], [f"xgo{b}"])
            dma(xT[:, off:off + W].rearrange("(c p) n -> p c n", p=128), xg[:, :, 0:W], [f"xgo{b}"], [("xT", g)], f"d_xs{b}")
            hb = h2[b]
            norm_mod(xg, W, A2, 3, s, hb, sq, rs, tmpb, ps_st, f"xgo{b}", "ps_st", rst)
            dma(h2T[:, off:off + W].rearrange("(c p) n -> p c n", p=128), hb[:, :, 0:W], ["hT"], [("h2T", g)], f"d_h2{b}")
        P.end()
        if stop_after == f"O1{l}":
            P.close()
            return nc

        w1s = P.sbuf("w1s", [128, 8, DFF], BF16)
        w3s = P.sbuf("w3s", [128, 8, DFF], BF16)
        for hh in range(2):
            for c in range(8):
                dma(w1s[:, c, hh * 1408:(hh + 1) * 1408], w1[l][c * 128:(c + 1) * 128, hh * 1408:(hh + 1) * 1408], [], [("w1s", hh)], f"d_w1{hh}", eng="pool")
            for c in range(8):
                dma(w3s[:, c, hh * 1408:(hh + 1) * 1408], w3[l][c * 128:(c + 1) * 128, hh * 1408:(hh + 1) * 1408], [], [("w3s", hh)], f"d_w3{hh}", eng="pool")
        h2i = [P.sbuf(f"h2i{i}", [128, 8, 512], BF16) for i in range(2)]
        sl = [P.sbuf(f"sl{i}", [128, 512], F32) for i in range(2)]
        gb = [P.sbuf(f"gb{i}", [128, 2, 512], BF16) for i in range(2)]
        ps_a = [P.psum(f"ps_a{i}", [128, 512]) for i in range(2)]
        ps_b = [P.psum(f"ps_b{i}", [128, 512]) for i in range(2)]

        def load_o2(gi):
            g, W, off, isctx = act_groups[gi]
            dma(h2i[gi % 2][:, :, 0:W], h2T[:, off:off + W].rearrange("(c p) n -> p c n", p=128), [("h2T", g)], [f"h2i{gi % 2}"], f"d_h2i{gi % 2}")

        load_o2(0)
        nj = 0
        for gi, (g, W, off, isctx) in enumerate(act_groups):
            if gi + 1 < len(act_groups):
                load_o2(gi + 1)
            hb = h2i[gi % 2]
            hk = f"h2i{gi % 2}"
            for j in range(DFF // 128):
                pa, pb = ps_a[j % 2], ps_b[j % 2]
                for kc in range(8):
                    mm(pa[:, 0:W], w1s[:, kc, j * 128:(j + 1) * 128], hb[:, kc, 0:W], kc == 0, kc == 7, [("w1s", j // 11), hk], [f"ps_a{j % 2}"])
                for kc in range(8):
                    mm(pb[:, 0:W], w3s[:, kc, j * 128:(j + 1) * 128], hb[:, kc, 0:W], kc == 0, kc == 7, [("w3s", j // 11), hk], [f"ps_b{j % 2}"])
                slb = sl[j % 2]
                act(slb[:, 0:W], pa[:, 0:W], AF.Silu, [f"ps_a{j % 2}"], [f"sl{j % 2}"])
                gbb = gb[(nj // 2) % 2]
                tt("dve", gbb[:, j % 2, 0:W], slb[:, 0:W], pb[:, 0:W], ALU.mult, [f"sl{j % 2}", f"ps_b{j % 2}"], [f"gb{(nj // 2) % 2}"])
                if j % 2 == 1:
                    dma(gT[(j - 1) * 128:(j + 1) * 128, off:off + W].rearrange("(c p) n -> p c n", p=128), gbb[:, :, 0:W], [f"gb{(nj // 2) % 2}"], [("gT", g)],
                        f"d_gb{(nj // 2) % 2}")
                nj += 1
        P.end()
        if stop_after == f"O2{l}":
            P.close()
            return nc

        w2s = P.sbuf("w2s", [128, DFF // 128, D], BF16)
        for j in range(DFF // 128):
            dma(w2s[:, j, :], w2[l][j * 128:(j + 1) * 128, :], [], [("w2s", j // 8)], f"d_w2{j // 8}", eng="pool")
        gi_b = [P.sbuf(f"gi{i}", [128, DFF // 128, 512], BF16) for i in range(2)]
        xg_f = [P.sbuf(f"xgf{i}", [128, 8, 512], F32) for i in range(2)]
        ot = [P.sbuf(f"ot{i}", [128, D], F32) for i in range(2)]
        ps_y = [P.psum(f"ps_y{i}", [128, 512]) for i in range(2)]
        ps_t = [P.psum(f"ps_t{i}", [128, 512]) for i in range(4)]

        def load_o3(gi):
            g, W, off, isctx = act_groups[gi]
            b = gi % 2
            for hh in range(2):
                dma(gi_b[b][:, hh * 11:(hh + 1) * 11, 0:W], gT[hh * 1408:(hh + 1) * 1408, off:off + W].rearrange("(c p) n -> p c n", p=128), [("gT", g)], [f"gi{b}"], f"d_gi{b}")
            dma(xg_f[b][:, :, 0:W], xT[:, off:off + W].rearrange("(c p) n -> p c n", p=128), [("xT", g)], [f"xgf{b}"], f"d_xgf{b}")

        load_o3(0)
        n = 0
        for gi, (g, W, off, isctx) in enumerate(act_groups):
            if gi + 1 < len(act_groups):
                load_o3(gi + 1)
            b = gi % 2
            s = 1 if isctx else 0
            xg = xg_f[b]
            for c in range(8):
                py = ps_y[c % 2]
                for j in range(DFF // 128):
                    mm(py[:, 0:W], w2s[:, j, c * 128:(c + 1) * 128], gi_b[b][:, j, 0:W], j == 0, j == DFF // 128 - 1, [("w2s", j // 8), f"gi{b}"], [f"ps_y{c % 2}"])
                stt(xg[:, c, 0:W], py[:, 0:W], Bv(5, c, s), xg[:, c, 0:W], ALU.mult, ALU.add, [f"ps_y{c % 2}", f"xgf{b}", "modT"], [f"xgf{b}"])
            if not last:
                dma(xT[:, off:off + W].rearrange("(c p) n -> p c n", p=128), xg[:, :, 0:W], [f"xgf{b}"], [("xT", g)], f"d_xs{b}")
            else:
                for t4 in range(W // 128):
                    otb = ot[n % 2]
                    for bb in range(2):
                        pb = ps_t[(n % 2) * 2 + bb]
                        for c4 in range(4):
                            c = bb * 4 + c4
                            tr(pb[:, c4 * 128:(c4 + 1) * 128], xg[:, c, t4 * 128:(t4 + 1) * 128], identf[:], [f"xgf{b}", "identf"], [f"ps_t{(n % 2) * 2 + bb}"])
                        cp("act" if bb == 0 else "dve", otb[:, bb * 512:(bb + 1) * 512], pb[:], [f"ps_t{(n % 2) * 2 + bb}"], [f"ot{n % 2}"])
                    dma(out[off + t4 * 128: off + (t4 + 1) * 128, :], otb[:], [f"ot{n % 2}"], [], f"d_ot{n % 2}")
                    n += 1
        P.end()
    P.close()
    return nc


def _rope_tables(S, half, NT):
    t = np.arange(half * NT, (half + 1) * NT, dtype=np.int32)
    row = (t // 64).astype(np.float32)
    col = (t % 64).astype(np.float32)
    inv = (np.float32(10000.0) ** (-np.arange(0, 32, 2, dtype=np.float32) / np.float32(32))).astype(np.float32)
    ar = row[:, None] * inv[None, :]
    ac = col[:, None] * inv[None, :]
    ang = np.concatenate([ar, ar, ac, ac], axis=-1)
    cos = np.cos(ang).astype(np.float32)
    sin = np.sin(ang).astype(np.float32)
    sign = np.ones(64, np.float32)
    for a in range(2):
        sign[a * 32:a * 32 + 16] = -1.0
    sin = sin * sign[None, :]
    cosT = np.ascontiguousarray(np.concatenate([cos.T, cos.T], axis=0))
    sinT = np.ascontiguousarray(np.concatenate([sin.T, sin.T], axis=0))
    return cosT, sinT


def _rperm():
    m = np.zeros((128, 128), np.float32)
    for base in (0, 64):
        for dp in range(64):
            tq = (dp % 32) // 16
            src = dp + 16 if tq == 0 else dp - 16
            m[base + src, base + dp] = 1.0
    return m


def _na_bias_tables(rpb, S, half, NT, NG):
    rows = S // 64
    NSET = 1 if NG == 1 else 3
    out = np.full((4, NSET * 8, 128, 512), -1e30, np.float32)
    kh = 8
    r_all = np.arange(rows)
    kr0 = np.clip(r_all - kh // 2, 0, rows - kh)
    cq = np.arange(64)
    c0 = np.clip(cq - 8, 0, 64 - 16)
    sets = [0] if NG == 1 else [0, 1, 2]
    for si, sset in enumerate(sets):
        if NG == 1:
            g = 0
        else:
            g = 0 if sset == 0 else (NG - 1 if sset == 2 else 1)
        R = half * (rows // 2) + g * 8
        qrow = R + np.arange(512) // 64
        qcol = np.arange(512) % 64
        for i in range(8):
            bt = g * 4 + i
            kt = half * (NT // 128) + bt - 2
            if kt < 0 or kt >= S // 128:
                continue
            krow = kt * 2 + np.arange(128) // 64
            kcol = np.arange(128) % 64
            rv = (krow[:, None] >= kr0[qrow][None, :]) & (krow[:, None] < kr0[qrow][None, :] + kh)
            cvd = (kcol[:, None] >= c0[qcol][None, :]) & (kcol[:, None] < c0[qcol][None, :] + 16)
            roff = np.clip(krow[:, None] - qrow[None, :] + 7, 0, 14)
            coff = np.clip(kcol[:, None] - qcol[None, :], -15, 15) + 15
            valid = rv & cvd
            for h in range(4):
                b = rpb[h][roff, coff]
                out[h, si * 8 + i] = np.where(valid, b, np.float32(-1e30))
    return out


def prepare_inputs(inp, S):
    NT = S // 2
    NG = NT // 512
    L = inp["w_mod"].shape[0]
    f = lambda a: np.ascontiguousarray(np.asarray(a, dtype=np.float32))
    x, c, ctx, c_ctx = f(inp["x"]), f(inp["c"]), f(inp["ctx"]), f(inp["c_ctx"])
    vecs = np.zeros((L, 128, 128), np.float32)
    for l in range(L):
        vecs[l, 0:48] = f(inp["b_mod"])[l].reshape(48, 128)
        vecs[l, 48:56] = f(inp["norm1_g"])[l].reshape(8, 128)
        vecs[l, 56:64] = f(inp["norm2_g"])[l].reshape(8, 128)
        vecs[l, 64] = np.tile(f(inp["na_q_g"])[l], 2)
        vecs[l, 65] = np.tile(f(inp["na_k_g"])[l], 2)
        vecs[l, 66] = np.tile(f(inp["da_q_g"])[l], 2)
        vecs[l, 67] = np.tile(f(inp["da_k_g"])[l], 2)
        vecs[l, 68] = f(inp["da_sub_g"])[l]
        vecs[l, 69:73] = f(inp["gm_bs"])[l]
    lamv = np.stack([np.concatenate([f(inp[k])[l] for k in ("da_lq1", "da_lk1", "da_lq2", "da_lk2")]) for l in range(L)])[:, None, :]
    vg = f(inp["gm_v_g"])[:, None, :]
    wsT = np.ascontiguousarray(np.transpose(f(inp["gm_ws"]), (0, 1, 3, 2)))
    common = dict(vecs=vecs, lamv=np.ascontiguousarray(lamv), vg=np.ascontiguousarray(vg), w_mod=f(inp["w_mod"]), w_in=f(inp["w_in"]),
                  w_out=f(inp["w_out"]), w1=f(inp["ffn_w1"]), w3=f(inp["ffn_w3"]), w2=f(inp["ffn_w2"]), wsT=wsT,
                  ident=np.eye(128, dtype=np.float32), rperm=_rperm(),
                  blk=np.kron(np.eye(2, dtype=np.float32), np.ones((64, 64), np.float32)))
    rpb = f(inp["na_rpb"])
    maps = []
    tabs = {}
    for core in range(8):
        b, half = core // 2, core % 2
        if half not in tabs:
            cosT, sinT = _rope_tables(S, half, NT)
            nabt = np.stack([_na_bias_tables(rpb[l], S, half, NT, NG) for l in range(L)])
            tabs[half] = (cosT, sinT, nabt)
        cosT, sinT, nabt = tabs[half]
        m = dict(common)
        m["x"] = np.ascontiguousarray(x[b, half * NT:(half + 1) * NT])
        m["ctx"] = np.ascontiguousarray(ctx[b])
        m["cvec"] = np.ascontiguousarray(np.concatenate([c[b].reshape(8, 128), c_ctx.reshape(8, 128)], axis=0))
        m["cos"], m["sin"], m["nab"] = cosT, sinT, nabt
        maps.append(m)
    return maps


_CACHE = {}


def kernel(**inputs):
    S = int(np.asarray(inputs["x"]).shape[1])
    B = int(np.asarray(inputs["x"]).shape[0])
    assert B == 4
    if S not in _CACHE:
        _CACHE[S] = build(S)
    nc = _CACHE[S]
    maps = prepare_inputs(inputs, S)
    res = run_bass_kernel_spmd(nc, maps, core_ids=list(range(8)))
    NT = S // 2
    out = np.empty((B, S, D), np.float32)
    for core in range(8):
        b, half = core // 2, core % 2
        out[b, half * NT:(half + 1) * NT] = res.results[core]["out"]
    return out
```
